# Optimizing a Trainium2 kernel written in Bass

```python
import math
import jax, jax.numpy as jnp
from jax import lax
import numpy as np

D_MODEL = 1024
BATCH = 8
SEQ = 2048
DEPTH = 4

CONV_CH = D_MODEL // 2
CONV_WIDTH = 31
DIFF_QK_DIM = 64
DIFF_V_DIM = 2 * DIFF_QK_DIM
DIFF_HEADS = D_MODEL // 256
MOBA_HEAD_DIM = 64
MOBA_HEADS = D_MODEL // 128
MOBA_BLOCK = 256
MOBA_TOPK = 3
MOBA_Q_CHUNK = 32
ATTN_Q_BLOCK = 128
D_FF = 4 * D_MODEL
ROPE_THETA = 10000.0
N_BRANCHES = 3
LN_EPS = 1e-5
NEG = -1e30
DEEPNORM_ALPHA = (2.0 * DEPTH) ** 0.25
DEEPNORM_BETA = (8.0 * DEPTH) ** -0.25

COLS_CONV = 2 * CONV_CH
COLS_DIFF_QK = 2 * DIFF_HEADS * DIFF_QK_DIM
COLS_DIFF_V = DIFF_HEADS * DIFF_V_DIM
COLS_MOBA = MOBA_HEADS * MOBA_HEAD_DIM
COLS_GATE = N_BRANCHES * D_MODEL
IN_COLS = COLS_CONV + 2 * COLS_DIFF_QK + COLS_DIFF_V + 3 * COLS_MOBA + COLS_GATE

kernel_name = "hybrid_conv_diffattn_moba_deepnorm"


def _split_offsets():
    sizes = [COLS_CONV, COLS_DIFF_QK, COLS_DIFF_QK, COLS_DIFF_V, COLS_MOBA, COLS_MOBA, COLS_MOBA]
    offs, acc = [], 0
    for s in sizes:
        acc += s
        offs.append(acc)
    return offs


def layer_norm(x, g, b):
    xf = x.astype(jnp.float32)
    mu = jnp.mean(xf, axis=-1, keepdims=True)
    var = jnp.mean(jnp.square(xf - mu), axis=-1, keepdims=True)
    y = (xf - mu) * lax.rsqrt(var + LN_EPS) * g.astype(jnp.float32) + b.astype(jnp.float32)
    return y.astype(x.dtype)


def rms_norm(x, g):
    xf = x.astype(jnp.float32)
    y = xf * lax.rsqrt(jnp.mean(jnp.square(xf), axis=-1, keepdims=True) + LN_EPS) * g.astype(jnp.float32)
    return y.astype(x.dtype)


def rope_tables(seq, dim):
    pos = jnp.arange(seq, dtype=jnp.float32)
    inv = ROPE_THETA ** (-jnp.arange(0, dim, 2, dtype=jnp.float32) / dim)
    ang = pos[:, None] * inv[None, :]
    ang = jnp.concatenate([ang, ang], axis=-1)
    return jnp.cos(ang), jnp.sin(ang)


def apply_rope(x, cos, sin):
    c = cos[None, :, None, :].astype(x.dtype)
    s = sin[None, :, None, :].astype(x.dtype)
    x1, x2 = jnp.split(x, 2, axis=-1)
    return x * c + jnp.concatenate([-x2, x1], axis=-1) * s


def conv_branch(u, conv_w, conv_b, ln_g, ln_b, w_out):
    a, gt = jnp.split(u, 2, axis=-1)
    h = a * jax.nn.sigmoid(gt)
    h = lax.conv_general_dilated(
        h, conv_w.astype(h.dtype), window_strides=(1,),
        padding=[(CONV_WIDTH - 1, 0)],
        dimension_numbers=("NWC", "WIO", "NWC"),
        feature_group_count=CONV_CH) + conv_b
    h = jax.nn.silu(layer_norm(h, ln_g, ln_b))
    return h @ w_out


def diff_attention(q, k, v, lq1, lk1, lq2, lk2, subln_g, layer_idx, cos, sin, w_out):
    B, S, _ = q.shape
    H, d = DIFF_HEADS, DIFF_QK_DIM
    q = apply_rope(q.reshape(B, S, 2 * H, d), cos, sin)
    k = apply_rope(k.reshape(B, S, 2 * H, d), cos, sin)
    q = q.transpose(0, 2, 1, 3).reshape(B, H, 2, S, d)
    k = k.transpose(0, 2, 1, 3).reshape(B, H, 2, S, d)
    v = v.reshape(B, S, H, DIFF_V_DIM).transpose(0, 2, 1, 3)
    lambda_init = 0.8 - 0.6 * math.exp(-0.3 * layer_idx)
    f32 = jnp.float32
    lam = (jnp.exp(jnp.sum(lq1.astype(f32) * lk1.astype(f32)))
           - jnp.exp(jnp.sum(lq2.astype(f32) * lk2.astype(f32))) + lambda_init)
    scale = d ** -0.5
    nqb = S // ATTN_Q_BLOCK
    qb = q.reshape(B, H, 2, nqb, ATTN_Q_BLOCK, d).transpose(3, 0, 1, 2, 4, 5)
    key_pos = jnp.arange(S)

    def one_block(args):
        qblk, bi = args
        s = jnp.einsum("bhtqd,bhtkd->bhtqk", qblk, k).astype(f32) * scale
        qpos = bi * ATTN_Q_BLOCK + jnp.arange(ATTN_Q_BLOCK)
        s = jnp.where(key_pos[None, :] <= qpos[:, None], s, NEG)
        p = jax.nn.softmax(s, axis=-1)
        a = p[:, :, 0] - lam * p[:, :, 1]
        return jnp.einsum("bhqk,bhkv->bhqv", a.astype(v.dtype), v)

    o = lax.map(one_block, (qb, jnp.arange(nqb)))
    o = o.transpose(1, 0, 3, 2, 4).reshape(B, S, H, DIFF_V_DIM)
    o = rms_norm(o, subln_g) * (1.0 - lambda_init)
    return o.reshape(B, S, H * DIFF_V_DIM) @ w_out


def moba_attention(q, k, v, cos, sin, w_out):
    B, S, _ = q.shape
    H, hd, BL, QC = MOBA_HEADS, MOBA_HEAD_DIM, MOBA_BLOCK, MOBA_Q_CHUNK
    q = apply_rope(q.reshape(B, S, H, hd), cos, sin).transpose(0, 2, 1, 3)
    k = apply_rope(k.reshape(B, S, H, hd), cos, sin).transpose(0, 2, 1, 3)
    v = v.reshape(B, S, H, hd).transpose(0, 2, 1, 3)
    nb = -(-S // BL)
    pad = nb * BL - S
    kb = jnp.pad(k, ((0, 0), (0, 0), (0, pad), (0, 0))).reshape(B, H, nb, BL, hd)
    vb = jnp.pad(v, ((0, 0), (0, 0), (0, pad), (0, 0))).reshape(B, H, nb, BL, hd)
    kmean = jnp.mean(kb, axis=3)
    topk = min(MOBA_TOPK, nb)
    scale = hd ** -0.5
    f32 = jnp.float32
    nch = S // QC
    qc = q.reshape(B, H, nch, QC, hd).transpose(2, 0, 1, 3, 4)
    bidx = jnp.arange(B)[:, None, None, None]
    hidx = jnp.arange(H)[None, :, None, None]
    blk_ids = jnp.arange(nb)

    def one_chunk(args):
        qch, ci = args
        start = ci * QC
        own = start // BL
        qpos = start + jnp.arange(QC)
        gate = jnp.einsum("bhqd,bhnd->bhqn", qch, kmean).astype(f32)
        gate = jnp.where(blk_ids < own, gate, NEG)
        _, idx = lax.top_k(gate, topk)
        valid = idx < own
        k_sel = kb[bidx, hidx, idx]
        v_sel = vb[bidx, hidx, idx]
        s_sel = jnp.einsum("bhqd,bhqnjd->bhqnj", qch, k_sel).astype(f32) * scale
        s_sel = jnp.where(valid[..., None], s_sel, NEG).reshape(B, H, QC, topk * BL)
        k_own = lax.dynamic_index_in_dim(kb, own, axis=2, keepdims=False)
        v_own = lax.dynamic_index_in_dim(vb, own, axis=2, keepdims=False)
        s_own = jnp.einsum("bhqd,bhjd->bhqj", qch, k_own).astype(f32) * scale
        kpos = own * BL + jnp.arange(BL)
        s_own = jnp.where(kpos[None, :] <= qpos[:, None], s_own, NEG)
        p = jax.nn.softmax(jnp.concatenate([s_sel, s_own], axis=-1), axis=-1).astype(v.dtype)
        p_sel = p[..., :topk * BL].reshape(B, H, QC, topk, BL)
        p_own = p[..., topk * BL:]
        return (jnp.einsum("bhqnj,bhqnjd->bhqd", p_sel, v_sel)
                + jnp.einsum("bhqj,bhjd->bhqd", p_own, v_own))

    o = lax.map(one_chunk, (qc, jnp.arange(nch)))
    o = o.transpose(1, 0, 3, 2, 4).reshape(B, S, H * hd)
    return o @ w_out


def setup_inputs(seed: int = 0) -> dict:
    key = jax.random.key(seed)
    ks = jax.random.split(key, 26)
    n = jax.random.normal
    L, D = DEPTH, D_MODEL
    beta = DEEPNORM_BETA
    return {
        "x": n(ks[0], (BATCH, SEQ, D), jnp.float32),
        "w_in": n(ks[1], (L, D, IN_COLS), jnp.float32) * D ** -0.5,
        "b_gate": 0.02 * n(ks[2], (L, COLS_GATE), jnp.float32),
        "conv_w": n(ks[3], (L, CONV_WIDTH, 1, CONV_CH), jnp.float32) * CONV_WIDTH ** -0.5,
        "conv_b": 0.02 * n(ks[4], (L, CONV_CH), jnp.float32),
        "conv_ln_g": 1.0 + 0.02 * n(ks[5], (L, CONV_CH), jnp.float32),
        "conv_ln_b": 0.02 * n(ks[6], (L, CONV_CH), jnp.float32),
        "w_conv_out": n(ks[7], (L, CONV_CH, D), jnp.float32) * CONV_CH ** -0.5 * beta,
        "lam_q1": 0.1 * n(ks[8], (L, DIFF_QK_DIM), jnp.float32),
        "lam_k1": 0.1 * n(ks[9], (L, DIFF_QK_DIM), jnp.float32),
        "lam_q2": 0.1 * n(ks[10], (L, DIFF_QK_DIM), jnp.float32),
        "lam_k2": 0.1 * n(ks[11], (L, DIFF_QK_DIM), jnp.float32),
        "diff_subln_g": 1.0 + 0.02 * n(ks[12], (L, DIFF_V_DIM), jnp.float32),
        "w_diff_out": n(ks[13], (L, COLS_DIFF_V, D), jnp.float32) * COLS_DIFF_V ** -0.5 * beta,
        "w_moba_out": n(ks[14], (L, COLS_MOBA, D), jnp.float32) * COLS_MOBA ** -0.5 * beta,
        "w_o": n(ks[15], (L, D, D), jnp.float32) * D ** -0.5 * beta,
        "ln1_g": 1.0 + 0.02 * n(ks[16], (L, D), jnp.float32),
        "ln1_b": 0.02 * n(ks[17], (L, D), jnp.float32),
        "w_ff1": n(ks[18], (L, D, D_FF), jnp.float32) * D ** -0.5 * beta,
        "b_ff1": 0.02 * n(ks[19], (L, D_FF), jnp.float32),
        "w_ff2": n(ks[20], (L, D_FF, D), jnp.float32) * D_FF ** -0.5 * beta,
        "b_ff2": 0.02 * n(ks[21], (L, D), jnp.float32),
        "ln2_g": 1.0 + 0.02 * n(ks[22], (L, D), jnp.float32),
        "ln2_b": 0.02 * n(ks[23], (L, D), jnp.float32),
    }


def reference(x, w_in, b_gate, conv_w, conv_b, conv_ln_g, conv_ln_b, w_conv_out,
              lam_q1, lam_k1, lam_q2, lam_k2, diff_subln_g, w_diff_out, w_moba_out, w_o,
              ln1_g, ln1_b, w_ff1, b_ff1, w_ff2, b_ff2, ln2_g, ln2_b):
    B, S, D = x.shape
    cos, sin = rope_tables(S, DIFF_QK_DIM)
    offs = _split_offsets()
    for l in range(DEPTH):
        u = x @ w_in[l]
        u_conv, dq, dk, dv, mq, mk, mv, g = jnp.split(u, offs, axis=-1)
        y_a = conv_branch(u_conv, conv_w[l], conv_b[l], conv_ln_g[l], conv_ln_b[l], w_conv_out[l])
        y_b = diff_attention(dq, dk, dv, lam_q1[l], lam_k1[l], lam_q2[l], lam_k2[l],
                             diff_subln_g[l], l, cos, sin, w_diff_out[l])
        y_c = moba_attention(mq, mk, mv, cos, sin, w_moba_out[l])
        gates = jax.nn.sigmoid(g + b_gate[l]).reshape(B, S, N_BRANCHES, D)
        m = gates[:, :, 0] * y_a + gates[:, :, 1] * y_b + gates[:, :, 2] * y_c
        x = layer_norm(DEEPNORM_ALPHA * x + m @ w_o[l], ln1_g[l], ln1_b[l])
        h = jnp.square(jax.nn.relu(x @ w_ff1[l] + b_ff1[l]))
        x = layer_norm(DEEPNORM_ALPHA * x + h @ w_ff2[l] + b_ff2[l], ln2_g[l], ln2_b[l])
    return x
```

```python
import math
import os
from contextlib import ExitStack
import numpy as np
import concourse.bass as bass
import concourse.mybir as mybir
from concourse.bass_utils import run_bass_kernel_spmd

F32 = mybir.dt.float32
BF16 = mybir.dt.bfloat16
AF = mybir.ActivationFunctionType
ALU = mybir.AluOpType
AX = mybir.AxisListType

COMPUTE = ("pe", "act", "dve", "pool")
ENGS = ("pe", "act", "dve", "pool", "sp")

S_LEN = 2048
D = 1024
DEPTH = 4
NT = 16
EPS = 1e-5
ALPHA = (2.0 * DEPTH) ** 0.25
MASKV = -30000.0
ARENA_BYTES = 212800


class Res:
    __slots__ = ("w", "r", "rd", "excl")

    def __init__(self, excl=False):
        self.w = None
        self.r = {}
        self.rd = []
        self.excl = excl


class Chan:
    __slots__ = ("sem", "count")

    def __init__(self, sem):
        self.sem = sem
        self.count = 0


class Op:
    __slots__ = ("eng", "fn", "cdeps", "ddeps", "idx", "signal", "seq", "chan", "is_dma")

    def __init__(self, eng, fn):
        self.eng = eng
        self.fn = fn
        self.cdeps = {}
        self.ddeps = {}
        self.signal = False
        self.seq = 0
        self.chan = None
        self.is_dma = False


class Sched:
    def __init__(self):
        self.ops = {e: [] for e in ENGS}
        self.pending_barrier = {e: None for e in ENGS}
        self.pending_chans = {e: None for e in ENGS}
        self.chans = []

    def _dep_on(self, op, d):
        if d is None or d is op:
            return
        if d.is_dma:
            ch = d.chan
            if op.is_dma and op.chan is ch:
                return
            op.ddeps[ch] = max(op.ddeps.get(ch, 0), ch.count)
        else:
            if d.eng == "pe" and op.eng == "pe" and not op.is_dma:
                return
            cur = op.cdeps.get(d.eng)
            if cur is None or cur.idx < d.idx:
                op.cdeps[d.eng] = d

    def add(self, eng, fn, reads=(), writes=(), chan=None):
        op = Op(eng, fn)
        op.idx = len(self.ops[eng])
        if chan is not None:
            op.is_dma = True
            op.chan = chan
        ex = [r for r in reads if r.excl]
        if ex:
            reads = [r for r in reads if not r.excl]
            writes = list(writes) + ex
        for r in reads:
            self._dep_on(op, r.w)
        for w in writes:
            self._dep_on(op, w.w)
            for d in w.r.values():
                self._dep_on(op, d)
            for d in w.rd:
                self._dep_on(op, d)
        pb = self.pending_barrier[eng]
        if pb is not None:
            for d in pb:
                self._dep_on(op, d)
            self.pending_barrier[eng] = None
            for ch, cnt in self.pending_chans[eng]:
                if cnt > 0 and not (op.is_dma and op.chan is ch):
                    op.ddeps[ch] = max(op.ddeps.get(ch, 0), cnt)
            self.pending_chans[eng] = None
        if chan is not None:
            chan.count += 16
        for r in reads:
            if op.is_dma:
                r.rd.append(op)
            else:
                r.r[eng] = op
        for w in writes:
            w.w = op
            w.r = {}
            w.rd = []
        self.ops[eng].append(op)
        return op

    def barrier(self):
        last = []
        for e in ENGS:
            if self.ops[e]:
                last.append(self.ops[e][-1])
        snap = [(ch, ch.count) for ch in self.chans]
        for e in ENGS:
            self.pending_barrier[e] = list(last)
            self.pending_chans[e] = snap

    def finalize(self):
        for e in ENGS:
            for op in self.ops[e]:
                for d in op.cdeps.values():
                    d.signal = True
        for e in ENGS:
            n = 0
            for op in self.ops[e]:
                if op.signal and not op.is_dma:
                    n += 1
                    op.seq = n

    def emit(self, eng, engine_obj, sems, final_waits=()):
        waited = {}
        for op in self.ops[eng]:
            for d in op.cdeps.values():
                s = sems[d.eng]
                if waited.get(id(s), 0) < d.seq:
                    engine_obj.wait_ge(s, d.seq)
                    waited[id(s)] = d.seq
            for ch, cnt in op.ddeps.items():
                if waited.get(id(ch.sem), 0) < cnt:
                    engine_obj.wait_ge(ch.sem, cnt)
                    waited[id(ch.sem)] = cnt
            ins = op.fn(engine_obj)
            if op.is_dma:
                ins.then_inc(op.chan.sem, 16)
            elif op.signal:
                ins.then_inc(sems[eng], 1)
        for ch in final_waits:
            if ch.count > 0 and waited.get(id(ch.sem), 0) < ch.count:
                engine_obj.wait_ge(ch.sem, ch.count)


class Arena:
    def __init__(self, tensor, nbytes):
        self.t = tensor
        self.nbytes = nbytes
        self.off = 0

    def alloc(self, shape, dtype, parts=128):
        esz = 2 if dtype == BF16 else 4
        n = 1
        for s in shape:
            n *= s
        nb = (n * esz + 31) // 32 * 32
        a = self.off
        self.off += nb
        assert self.off <= self.nbytes, ("arena overflow", self.off, self.nbytes)
        ap = self.t[0:parts, a // 4:(a + nb) // 4]
        if dtype != F32:
            ap = ap.bitcast(dtype)
        ap = ap[:, 0:n]
        if len(shape) == 2:
            ap = ap.rearrange("p (a b) -> p a b", b=shape[1])
        elif len(shape) == 3:
            ap = ap.rearrange("p (a b c) -> p a b c", b=shape[1], c=shape[2])
        return ap


class Rot:
    def __init__(self, items):
        self.items = items
        self.i = 0

    def next(self):
        v = self.items[self.i % len(self.items)]
        self.i += 1
        return v


def build_program(n_layers=DEPTH, stop=99):
    nc = bass.Bass("TRN2", target_bir_lowering=False)

    def din(name, shape):
        return nc.dram_tensor(name, list(shape), F32, kind="ExternalInput").ap()

    x_d = din("x", [S_LEN, D])
    w_in_d = din("w_in", [DEPTH, D, 7168])
    w_co_d = din("w_conv_out", [DEPTH, 512, D])
    w_do_d = din("w_diff_out", [DEPTH, 512, D])
    w_mo_d = din("w_moba_out", [DEPTH, 512, D])
    w_o_d = din("w_o", [DEPTH, D, D])
    w_f1_d = din("w_ff1", [DEPTH, D, 4096])
    w_f2_d = din("w_ff2", [DEPTH, 4096, D])
    bgate_d = din("b_gate_t", [DEPTH, 128, 24])
    convw_d = din("conv_w_t", [DEPTH, 128, 4 * 31])
    convp_d = din("conv_p_t", [DEPTH, 128, 12])
    lam_d = din("lam_all", [DEPTH, 256])
    subg_d = din("subln_g_t", [DEPTH, 128, 1])
    ln_d = din("ln_all", [DEPTH, 4, D])
    bff1_d = din("b_ff1_t", [DEPTH, 128, 32])
    bff2_d = din("b_ff2", [DEPTH, D])
    cos_d = din("cos_t", [128, S_LEN])
    sin_d = din("sin_t", [128, S_LEN])
    cst_d = din("cst", [128, 6 * 128])
    ind_d = din("ind_h", [8, S_LEN])
    vmask_d = din("vmask", [128, 256])
    y_d = nc.dram_tensor("y", [S_LEN, D], F32, kind="ExternalOutput").ap()

    es = ExitStack()
    with es:
        arena_t = es.enter_context(nc.sbuf_tensor("arena", [128, ARENA_BYTES // 4], F32))
        banks_t = [es.enter_context(nc.psum_tensor(f"ps{i}", [128, 512], F32)) for i in range(8)]
        sems = {e: es.enter_context(nc.semaphore(f"s_{e}")) for e in ENGS}
        _nch = [0]

        def new_chan():
            _nch[0] += 1
            c = Chan(es.enter_context(nc.semaphore(f"ch{_nch[0]}")))
            S.chans.append(c)
            return c

        S = Sched()
        banks = [b[:, :] for b in banks_t]
        Rb = [Res(excl=True) for _ in range(8)]

        P = Arena(arena_t, ARENA_BYTES)
        xT = P.alloc((8, S_LEN), BF16)
        R_xT = [Res() for _ in range(NT)]
        cst = P.alloc((6, 128), BF16)
        ident, Rm, tri, ones, onesE, onesO = [cst[:, i, :] for i in range(6)]
        R_cst = Res()
        vmask = P.alloc((2, 16, 8), F32)
        biaspad = P.alloc((16, 72), BF16)
        R_biaspad = Res()
        bgate = P.alloc((24,), F32)
        convw = P.alloc((4, 31), F32)
        convp = P.alloc((3, 4), F32)
        lamt = P.alloc((4, 64), F32)
        subg = P.alloc((1,), F32)
        bff1 = P.alloc((32,), F32)
        b2row = P.alloc((D,), BF16)
        lamp = P.alloc((2, 64), F32)
        lams = P.alloc((4,), F32)
        neglam = P.alloc((1,), F32)
        gsc = P.alloc((1,), F32)
        R_small = Res()
        R_lam = Res()
        NST = 4
        stt_l = [P.alloc((12,), F32) for _ in range(NST)]
        mv_l = [P.alloc((2,), F32) for _ in range(NST)]
        rstd1_l = [P.alloc((1,), F32) for _ in range(NST)]
        nmr1_l = [P.alloc((1,), F32) for _ in range(NST)]
        R_st_l = [Res() for _ in range(NST)]
        X0 = P.off
        XBYTES = 65536
        Y0 = X0 + XBYTES
        ch_small = new_chan()
        ch_y = new_chan()
        ch_misc = new_chan()
        ch_misc_sp = new_chan()
        ch_small_pool = new_chan()
        ch_trig = new_chan()
        R_b2 = Res()
        R_y = [Res() for _ in range(NT)]

        def region(off):
            a = Arena(arena_t, ARENA_BYTES)
            a.off = off
            return a

        XA = region(X0)
        oT = [XA.alloc((4, S_LEN), BF16) for _ in range(3)]
        cosT = XA.alloc((S_LEN,), F32)
        sinT = XA.alloc((S_LEN,), F32)
        assert XA.off <= Y0
        R_oT = [[[Res() for _ in range(4)] for _ in range(4)] for _ in range(3)]
        R_trig = Res()
        XB = region(X0)
        x_res = XB.alloc((NT, D), F32)
        R_xres = [Res() for _ in range(NT)]
        XC = region(X0 + 16384)
        v32 = XC.alloc((4, S_LEN), F32)
        R_v32 = [[Res() for _ in range(4)] for _ in range(4)]

        def dma(q, out, in_, reads=(), writes=(), chan=None):
            S.add(q, lambda e: e.dma_start(out=out, in_=in_), reads=reads, writes=writes, chan=chan)

        def mm(out, lhsT, rhs, start, stop, reads, writes):
            S.add("pe", lambda e: e.matmul(out, lhsT, rhs, start=start, stop=stop), reads=reads, writes=writes)

        dma("pool", cst.rearrange("p a b -> p (a b)"), cst_d, writes=[R_cst], chan=ch_misc)
        dma("sp", vmask.rearrange("p a b c -> p (a b c)"), vmask_d, writes=[R_cst], chan=ch_misc_sp)
        S.add("pool", lambda e: e.memset(biaspad, 0.0), writes=[R_biaspad])

        proj_pool = Rot([0, 1, 2])
        aux_pool = Rot([3, 4, 5, 6])

        def make_xT(tt, xb_ap, R_xb):
            b = aux_pool.next()
            psT = banks[b].bitcast(BF16)
            for k in range(8):
                S.add("pe", lambda e, k=k: e.transpose(psT[:, k * 128:(k + 1) * 128], xb_ap[:, k * 128:(k + 1) * 128], ident),
                      reads=[R_xb, R_cst], writes=[Rb[b]])
            if os.environ.get("DBG_XT"):
                S.add("dve", lambda e: e.tensor_copy(xT[:, :, tt * 128:(tt + 1) * 128],
                                                     psT.rearrange("p (k t) -> p k t", t=128)),
                      reads=[Rb[b]], writes=[R_xT[tt]])
            else:
                S.add("act", lambda e: e.copy(xT[:, :, tt * 128:(tt + 1) * 128],
                                              psT.rearrange("p (k t) -> p k t", t=128)),
                      reads=[Rb[b]], writes=[R_xT[tt]])

        Y = region(Y0)
        xb0 = [Y.alloc((D,), BF16) for _ in range(2)]
        R_xb0 = [Res(), Res()]
        ch_xb0 = [new_chan(), new_chan()]
        for tt in range(NT):
            s = tt % 2
            dma("pool", xb0[s], x_d[tt * 128:(tt + 1) * 128, :], writes=[R_xb0[s]], chan=ch_xb0[s])
            make_xT(tt, xb0[s], R_xb0[s])
        S.barrier()

        def layer_norm_tile(tt, lng, lnb, R_ln, write_out, make_next, xbs, R_xbs, pipe):
            xr = x_res[:, tt, :]
            R = R_xres[tt]
            stt, mv, rstd1, nmr1, R_st = stt_l[tt % NST], mv_l[tt % NST], rstd1_l[tt % NST], nmr1_l[tt % NST], R_st_l[tt % NST]
            S.add("dve", lambda e: e.bn_stats(stt[:, 0:6], xr[:, 0:512]), reads=[R], writes=[R_st])
            S.add("dve", lambda e: e.bn_stats(stt[:, 6:12], xr[:, 512:1024]), reads=[R], writes=[R_st])
            S.add("dve", lambda e: e.bn_aggr(mv, stt), reads=[R_st], writes=[R_st])
            S.add("act", lambda e: e.activation(rstd1, mv[:, 1:2], AF.Ln, bias=EPS, scale=1.0), reads=[R_st], writes=[R_st])
            S.add("act", lambda e: e.activation(rstd1, rstd1, AF.Exp, scale=-0.5), reads=[R_st], writes=[R_st])
            S.add("dve", lambda e: e.scalar_tensor_tensor(nmr1, mv[:, 0:1], -1.0, rstd1, ALU.mult, ALU.mult),
                  reads=[R_st], writes=[R_st])
            S.add("act", lambda e: e.activation(xr, xr, AF.Identity, bias=nmr1, scale=rstd1), reads=[R_st, R], writes=[R])
            s = tt % 2

            def stage_b():
                S.add("dve", lambda e: e.tensor_tensor(xr[:, 0:512], xr[:, 0:512], lng[:, 0:512], ALU.mult), reads=[R, R_ln], writes=[R])
                S.add("pool", lambda e: e.tensor_tensor(xr[:, 512:1024], xr[:, 512:1024], lng[:, 512:1024], ALU.mult), reads=[R, R_ln], writes=[R])
                S.add("pool", lambda e: e.tensor_tensor(xr, xr, lnb, ALU.add), reads=[R, R_ln], writes=[R])
                if write_out:
                    dma("sp", y_d[tt * 128:(tt + 1) * 128, :], xr, reads=[R], writes=[R_y[tt]], chan=ch_y)
                if make_next:
                    S.add("act", lambda e: e.copy(xbs[s], xr), reads=[R], writes=[R_xbs[s]])

            def stage_c():
                if make_next:
                    make_xT(tt, xbs[s], R_xbs[s])

            pipe.append([stage_b, stage_c])

        def run_pipe(pipe, drain=False):
            while True:
                for item in list(pipe):
                    item.pop(0)()
                    if not item:
                        pipe.remove(item)
                if not drain or not pipe:
                    break

        for l in range(n_layers):
            if stop == 0:
                break
            lam_init = 0.8 - 0.6 * math.exp(-0.3 * l)
            last_layer = (l == n_layers - 1)
            w_in_l = w_in_d[l].rearrange("(k p) n -> p k n", p=128)

            dma("sp", bgate, bgate_d[l], writes=[R_small], chan=ch_small)
            dma("sp", convw.rearrange("p a b -> p (a b)"), convw_d[l], writes=[R_small], chan=ch_small)
            dma("sp", convp.rearrange("p a b -> p (a b)"), convp_d[l], writes=[R_small], chan=ch_small)
            dma("sp", lamt.rearrange("p a b -> p (a b)"), lam_d[l].partition_broadcast(128), writes=[R_small], chan=ch_small)
            dma("sp", subg, subg_d[l], writes=[R_small], chan=ch_small)
            dma("sp", bff1, bff1_d[l], writes=[R_small], chan=ch_small)
            dma("pool", b2row[0:1, :], bff2_d[l:l + 1, :], writes=[R_b2], chan=ch_small_pool)
            dma("sp", cosT, cos_d, writes=[R_trig], chan=ch_trig)
            dma("sp", sinT, sin_d, writes=[R_trig], chan=ch_trig)
            S.add("dve", lambda e: e.tensor_tensor(lamp[:, 0, :], lamt[:, 0, :], lamt[:, 1, :], ALU.mult), reads=[R_small], writes=[R_lam])
            S.add("dve", lambda e: e.tensor_tensor(lamp[:, 1, :], lamt[:, 2, :], lamt[:, 3, :], ALU.mult), reads=[R_small], writes=[R_lam])
            S.add("dve", lambda e: e.tensor_reduce(lams[:, 0:2], lamp, AX.X, ALU.add), reads=[R_lam], writes=[R_lam])
            S.add("act", lambda e: e.activation(lams[:, 2:4], lams[:, 0:2], AF.Exp), reads=[R_lam], writes=[R_lam])
            S.add("dve", lambda e: e.tensor_tensor(neglam, lams[:, 3:4], lams[:, 2:3], ALU.subtract), reads=[R_lam], writes=[R_lam])
            S.add("dve", lambda e, li=lam_init: e.tensor_scalar(neglam, neglam, -li, None, ALU.add), reads=[R_lam], writes=[R_lam])
            S.add("dve", lambda e, li=lam_init: e.tensor_scalar(gsc, subg, (1.0 - li) * math.sqrt(128.0), None, ALU.mult),
                  reads=[R_small], writes=[R_lam])

            Y = region(Y0)
            wA = [Y.alloc((8, 128), BF16) for _ in range(2)]
            wG = [Y.alloc((8, 128), BF16) for _ in range(2)]
            R_wc = [Res(), Res()]
            ch_wc = [new_chan(), new_chan()]
            hT = [Y.alloc((2080,), BF16) for _ in range(2)]
            R_hT = [[Res() for _ in range(5)] for _ in range(2)]
            diag2 = [Y.alloc((31, 128), BF16) for _ in range(2)]
            R_diag2 = [Res(), Res()]
            sg = [Y.alloc((512,), F32) for _ in range(2)]
            R_sg = [Res(), Res()]
            YT0 = ARENA_BYTES - 18432
            YT = region(YT0)
            vb = [YT.alloc((512,), BF16) for _ in range(4)]
            sq = [YT.alloc((512,), BF16) for _ in range(4)]
            R_vb = [Res() for _ in range(4)]
            R_sq = [Res() for _ in range(4)]
            mean = YT.alloc((512,), F32)
            msq = YT.alloc((512,), F32)
            var = YT.alloc((512,), F32)
            zt = [YT.alloc((512,), F32) for _ in range(2)]
            assert Y.off <= YT0
            R_mean, R_var = Res(), Res()
            R_zt = [Res(), Res()]

            for s in range(2):
                S.add("pool", lambda e, s=s: e.memset(hT[s][:, 0:30], 0.0), writes=[R_hT[s][0]])
            for cr in range(4):
                s = cr % 2
                dma("pool", wA[s], w_in_l[:, :, cr * 128:(cr + 1) * 128], writes=[R_wc[s]], chan=ch_wc[s])
                dma("pool", wG[s], w_in_l[:, :, 512 + cr * 128:512 + (cr + 1) * 128], writes=[R_wc[s]], chan=ch_wc[s])
                diag = diag2[s]
                R_diag = R_diag2[s]
                S.add("dve", lambda e, cr=cr, diag=diag: e.tensor_tensor(
                    diag, ident.rearrange("p (o c) -> p o c", o=1).broadcast_to([128, 31, 128]),
                    convw[:, cr, :].rearrange("p (j o) -> p j o", o=1).broadcast_to([128, 31, 128]), ALU.mult),
                    reads=[R_cst, R_small], writes=[R_diag])
                for tc in range(4):
                    bA = proj_pool.next()
                    for k in range(8):
                        mm(banks[bA], wA[s][:, k, :], xT[:, k, tc * 512:(tc + 1) * 512], k == 0, k == 7,
                           [R_wc[s]] + R_xT[tc * 4:tc * 4 + 4], [Rb[bA]])
                    bG = proj_pool.next()
                    for k in range(8):
                        mm(banks[bG], wG[s][:, k, :], xT[:, k, tc * 512:(tc + 1) * 512], k == 0, k == 7,
                           [R_wc[s]] + R_xT[tc * 4:tc * 4 + 4], [Rb[bG]])
                    g2 = tc % 2
                    S.add("act", lambda e, bG=bG, g2=g2: e.activation(sg[g2], banks[bG], AF.Sigmoid), reads=[Rb[bG]], writes=[R_sg[g2]])
                    S.add("dve", lambda e, bA=bA, g2=g2, s=s, tc=tc: e.tensor_tensor(
                        hT[s][:, 30 + tc * 512:30 + (tc + 1) * 512], banks[bA], sg[g2], ALU.mult),
                        reads=[Rb[bA], R_sg[g2]], writes=[R_hT[s][tc + 1]])
                for tc in range(4):
                    bC = aux_pool.next()
                    for j in range(31):
                        mm(banks[bC], diag[:, j, :], hT[s][:, tc * 512 + j:tc * 512 + j + 512], j == 0, j == 30,
                           [R_diag, R_hT[s][tc], R_hT[s][tc + 1]], [Rb[bC]])
                    S.add("act", lambda e, bC=bC, cr=cr, tc=tc: e.activation(
                        v32[:, cr, tc * 512:(tc + 1) * 512], banks[bC], AF.Identity, bias=convp[:, 0, cr:cr + 1], scale=1.0),
                        reads=[Rb[bC], R_small], writes=[R_v32[cr][tc]])
            R_alias = [R_oT[b][k][t] for b in (1, 2) for k in range(4) for t in range(4)]

            def convln_p1(tc):
                for cr in range(4):
                    src = v32[:, cr, tc * 512:(tc + 1) * 512]
                    S.add("act", lambda e, cr=cr, src=src: e.copy(vb[cr], src), reads=[R_v32[cr][tc]] + R_alias, writes=[R_vb[cr]])
                    S.add("act", lambda e, cr=cr, src=src: e.activation(sq[cr], src, AF.Square), reads=[R_v32[cr][tc]] + R_alias, writes=[R_sq[cr]])

            def convln_p2(tc):
                bM = aux_pool.next()
                for cr in range(4):
                    mm(banks[bM], ones, vb[cr], cr == 0, cr == 3, [R_cst, R_vb[cr]], [Rb[bM]])
                bQ = aux_pool.next()
                for cr in range(4):
                    mm(banks[bQ], ones, sq[cr], cr == 0, cr == 3, [R_cst, R_sq[cr]], [Rb[bQ]])
                S.add("dve", lambda e: e.tensor_scalar(mean, banks[bM], 1.0 / 512, None, ALU.mult), reads=[Rb[bM]], writes=[R_mean])
                S.add("dve", lambda e: e.tensor_tensor(msq, mean, mean, ALU.mult), reads=[R_mean], writes=[R_var])
                S.add("dve", lambda e: e.scalar_tensor_tensor(var, banks[bQ], 1.0 / 512, msq, ALU.mult, ALU.subtract),
                      reads=[Rb[bQ], R_var], writes=[R_var])
                S.add("act", lambda e: e.activation(var, var, AF.Ln, bias=EPS, scale=1.0), reads=[R_var], writes=[R_var])
                S.add("act", lambda e: e.activation(var, var, AF.Exp, scale=-0.5), reads=[R_var], writes=[R_var])
                for cr in range(4):
                    z = zt[cr % 2]
                    Rz = R_zt[cr % 2]
                    src = v32[:, cr, tc * 512:(tc + 1) * 512]
                    S.add("dve", lambda e, z=z, src=src: e.tensor_tensor(z, src, mean, ALU.subtract), reads=[R_v32[cr][tc], R_mean] + R_alias, writes=[Rz])
                    S.add("dve", lambda e, z=z: e.tensor_tensor(z, z, var, ALU.mult), reads=[Rz, R_var], writes=[Rz])
                    S.add("act", lambda e, z=z, cr=cr: e.activation(
                        oT[0][:, cr, tc * 512:(tc + 1) * 512], z, AF.Silu, bias=convp[:, 2, cr:cr + 1], scale=convp[:, 1, cr:cr + 1]),
                        reads=[Rz, R_small], writes=[R_oT[0][cr][tc]])
            S.barrier()
            if stop == 1:
                break

            Y = region(Y0)
            wq = [Y.alloc((8, 128), BF16) for _ in range(2)]
            wk = [Y.alloc((8, 128), BF16) for _ in range(2)]
            wv = [Y.alloc((8, 128), BF16) for _ in range(2)]
            R_wj = [Res(), Res()]
            ch_wj = [new_chan(), new_chan()]
            QTa = Y.alloc((S_LEN,), BF16)
            QTb = Y.alloc((S_LEN,), BF16)
            KTa = Y.alloc((S_LEN,), BF16)
            KTb = Y.alloc((S_LEN,), BF16)
            R_QTa = [Res() for _ in range(4)]
            R_QTb = [Res() for _ in range(4)]
            R_KTa = [Res() for _ in range(4)]
            R_KTb = [Res() for _ in range(4)]
            ch_ind = new_chan()
            VA = Y.alloc((16, 128), BF16)
            VB = Y.alloc((16, 128), BF16)
            R_VA = [Res() for _ in range(4)]
            R_VB = [Res() for _ in range(4)]
            PT = [Y.alloc((512,), BF16) for _ in range(4)]
            R_PT = [Res() for _ in range(4)]
            pt_rot = Rot([0, 1, 2, 3])
            ubf = [Y.alloc((512,), BF16) for _ in range(2)]
            t1 = [Y.alloc((512,), F32) for _ in range(2)]
            t2 = [Y.alloc((512,), F32) for _ in range(2)]
            R_ubf = [Res(), Res()]
            R_t1 = [Res(), Res()]
            R_t2 = [Res(), Res()]
            lnbuf = [Y.alloc((512,), F32) for _ in range(2)]
            rcp = [Y.alloc((512,), F32) for _ in range(2)]
            R_rcp = [Res(), Res()]
            a1 = Y.alloc((512,), F32)
            a2 = Y.alloc((512,), F32)
            ob = [Y.alloc((512,), F32) for _ in range(2)]
            sqo = [Y.alloc((512,), BF16) for _ in range(2)]
            rst = [Y.alloc((512,), F32) for _ in range(2)]
            R_a1, R_a2 = Res(), Res()
            R_ob = [Res(), Res()]
            R_sqo = [Res(), Res()]
            R_rst = [Res(), Res()]
            Gt = Y.alloc((32, 8), F32)
            cmp = Y.alloc((32, 8, 8), BF16)
            rank = Y.alloc((32, 8), F32)
            kmf = Y.alloc((8,), F32)
            kmT = Y.alloc((8,), BF16)
            R_gate = Res()
            R_km = Res()

            S.add("pool", lambda e: e.memset(QTa, 0.0), writes=R_QTa)
            S.add("pool", lambda e: e.memset(QTb, 0.0), writes=R_QTb)
            S.add("pool", lambda e: e.memset(KTa[64:128, :], 0.0), writes=R_KTa)
            S.add("pool", lambda e: e.memset(KTb[0:64, :], 0.0), writes=R_KTb)

            jobs = [("diff", h) for h in range(4)] + [("moba", jp) for jp in range(4)]

            def job_cols(job):
                kind, i = job
                if kind == "diff":
                    return 1024 + i * 128, 1536 + i * 128, 2048 + i * 128
                return 2560 + i * 128, 3072 + i * 128, 3584 + i * 128

            def load_job_weights(ji):
                s = ji % 2
                cq, ck, cv = job_cols(jobs[ji])
                dma("pool", wq[s], w_in_l[:, :, cq:cq + 128], writes=[R_wj[s]], chan=ch_wj[s])
                dma("pool", wk[s], w_in_l[:, :, ck:ck + 128], writes=[R_wj[s]], chan=ch_wj[s])
                dma("pool", wv[s], w_in_l[:, :, cv:cv + 128], writes=[R_wj[s]], chan=ch_wj[s])

            assert Y.off <= YT0, (Y.off, YT0)
            load_job_weights(0)
            on_pairs = Rot([(3, 4), (5, 6)])
            s_pool = Rot([0, 1, 2])
            pt_rot = Rot([0, 1, 2, 3])

            for ji, job in enumerate(jobs):
                kind, idx = job
                s = ji % 2
                if ji + 1 < len(jobs):
                    load_job_weights(ji + 1)
                if kind == "moba" and idx == 0:
                    dma("pool", KTa[64:72, :], ind_d, writes=R_KTa, chan=ch_ind)
                    dma("pool", KTb[0:8, :], ind_d, writes=R_KTb, chan=ch_ind)
                    S.add("pool", lambda e: e.memset(VA[:, :, 64:128], 1.0), writes=R_VA)
                    S.add("pool", lambda e: e.memset(VB[:, :, 0:64], 1.0), writes=R_VB)
                tiles = [("q", wq[s], tc) for tc in range(4)] + [("k", wk[s], tc) for tc in range(4)]
                pend = None

                def rope_tail(p):
                    bU, u2, which, tc = p
                    bR = aux_pool.next()
                    mm(banks[bR], Rm, ubf[u2], True, True, [R_cst, R_ubf[u2]], [Rb[bR]])
                    S.add("dve", lambda e: e.tensor_tensor(t1[u2], banks[bU], cosT[:, tc * 512:(tc + 1) * 512], ALU.mult),
                          reads=[Rb[bU], R_trig], writes=[R_t1[u2]])
                    S.add("dve", lambda e: e.tensor_tensor(t2[u2], banks[bR], sinT[:, tc * 512:(tc + 1) * 512], ALU.mult),
                          reads=[Rb[bR], R_trig], writes=[R_t2[u2]])
                    cs = slice(tc * 512, (tc + 1) * 512)
                    if which == "q" and kind == "diff":
                        S.add("dve", lambda e: e.tensor_tensor(QTa[:, cs], t1[u2], t2[u2], ALU.add),
                              reads=[R_t1[u2], R_t2[u2]], writes=[R_QTa[tc]])
                    else:
                        da, db = (QTa, QTb) if which == "q" else (KTa, KTb)
                        Ra, Rb_ = (R_QTa, R_QTb) if which == "q" else (R_KTa, R_KTb)
                        S.add("dve", lambda e: e.tensor_tensor(da[0:64, cs], t1[u2][0:64, :], t2[u2][0:64, :], ALU.add),
                              reads=[R_t1[u2], R_t2[u2]], writes=[Ra[tc]])
                        S.add("dve", lambda e: e.tensor_tensor(db[64:128, cs], t1[u2][64:128, :], t2[u2][64:128, :], ALU.add),
                              reads=[R_t1[u2], R_t2[u2]], writes=[Rb_[tc]])

                if ji == 0 and os.environ.get("DBG_NOINT"):
                    for tcc in range(4):
                        convln_p1(tcc)
                        convln_p2(tcc)
                elif ji == 0:
                    convln_p1(0)
                for ti, (which, w, tc) in enumerate(tiles):
                    if ji == 0 and 1 <= ti <= 4 and not os.environ.get("DBG_NOINT"):
                        convln_p2(ti - 1)
                        if ti < 4:
                            convln_p1(ti)
                    bU = proj_pool.next()
                    for k in range(8):
                        mm(banks[bU], w[:, k, :], xT[:, k, tc * 512:(tc + 1) * 512], k == 0, k == 7,
                           [R_wj[s]] + R_xT[tc * 4:tc * 4 + 4], [Rb[bU]])
                    u2 = ti % 2
                    S.add("act", lambda e, bU=bU, u2=u2: e.copy(ubf[u2], banks[bU]), reads=[Rb[bU]], writes=[R_ubf[u2]])
                    if pend is not None:
                        rope_tail(pend)
                    pend = (bU, u2, which, tc)
                for g4 in range(4):
                    bV = aux_pool.next()
                    for t4 in range(4):
                        tt = g4 * 4 + t4
                        for k in range(8):
                            mm(banks[bV][:, t4 * 128:(t4 + 1) * 128], xT[:, k, tt * 128:(tt + 1) * 128], wv[s][:, k, :],
                               k == 0, k == 7, [R_wj[s], R_xT[tt]], [Rb[bV]])
                    if g4 == 0:
                        rope_tail(pend)
                        pend = None
                    pv3 = banks[bV].rearrange("p (a b) -> p a b", b=128)
                    if kind == "diff":
                        S.add("dve", lambda e, g4=g4, pv3=pv3: e.tensor_copy(VA[:, g4 * 4:(g4 + 1) * 4, :], pv3),
                              reads=[Rb[bV]], writes=[R_VA[g4]])
                    else:
                        S.add("dve", lambda e, g4=g4, pv3=pv3: e.tensor_copy(VA[:, g4 * 4:(g4 + 1) * 4, 0:64], pv3[:, :, 0:64]),
                              reads=[Rb[bV]], writes=[R_VA[g4]])
                        S.add("dve", lambda e, g4=g4, pv3=pv3: e.tensor_copy(VB[:, g4 * 4:(g4 + 1) * 4, 64:128], pv3[:, :, 64:128]),
                              reads=[Rb[bV]], writes=[R_VB[g4]])
                if kind == "moba":
                    S.add("dve", lambda e: e.tensor_reduce(kmf[0:64, :], KTa[0:64, :].rearrange("p (n j) -> p n j", j=256), AX.X, ALU.add),
                          reads=R_KTa, writes=[R_km])
                    S.add("dve", lambda e: e.tensor_reduce(kmf[64:128, :], KTb[64:128, :].rearrange("p (n j) -> p n j", j=256), AX.X, ALU.add),
                          reads=R_KTb, writes=[R_km])
                    S.add("dve", lambda e: e.tensor_scalar(kmT, kmf, 1.0 / 256, None, ALU.mult), reads=[R_km], writes=[R_km])
                    for qt in range(16):
                        for hl in range(2):
                            Qs = QTa if hl == 0 else QTb
                            Rq = R_QTa if hl == 0 else R_QTb
                            mm(banks[7][:, qt * 16 + hl * 8:qt * 16 + hl * 8 + 8],
                               Qs[hl * 64:(hl + 1) * 64, qt * 128:(qt + 1) * 128], kmT[hl * 64:(hl + 1) * 64, :],
                               True, True, [Rq[qt // 4], R_km], [Rb[7]])
                    G4 = Gt.rearrange("p (a h) n -> p a h n", h=2)
                    S.add("dve", lambda e, G4=G4: e.tensor_tensor(
                        G4, banks[7][:, 0:256].rearrange("p (a h n) -> p a h n", h=2, n=8),
                        vmask[:, 0, :, :].rearrange("p a (o n) -> p a o n", o=1).broadcast_to([128, 16, 2, 8]), ALU.add),
                        reads=[Rb[7], R_cst], writes=[R_gate])
                    ga = Gt.rearrange("p a (o n) -> p a o n", o=1).broadcast_to([128, 32, 8, 8])
                    gb = Gt.rearrange("p a (n o) -> p a n o", o=1).broadcast_to([128, 32, 8, 8])
                    S.add("dve", lambda e, ga=ga, gb=gb: e.tensor_tensor(cmp, ga, gb, ALU.is_gt), reads=[R_gate], writes=[R_gate])
                    S.add("dve", lambda e: e.tensor_reduce(rank, cmp, AX.X, ALU.add), reads=[R_gate], writes=[R_gate])
                    R4 = rank.rearrange("p (a h) n -> p a h n", h=2)
                    for hl in range(2):
                        c0 = 64 if hl == 0 else 0
                        S.add("dve", lambda e, hl=hl, R4=R4, c0=c0: e.scalar_tensor_tensor(
                            biaspad[:, :, c0:c0 + 8], R4[:, :, hl, :], 3.0, vmask[:, 1, :, :], ALU.is_ge, ALU.mult),
                            reads=[R_gate, R_cst], writes=[R_biaspad])
                    for q4 in range(4):
                        bB = aux_pool.next()
                        for t4 in range(4):
                            qt = q4 * 4 + t4
                            mm(banks[bB][0:72, t4 * 128:(t4 + 1) * 128], biaspad[:, qt, :], ident, True, True,
                               [R_biaspad, R_cst], [Rb[bB]])
                        S.add("dve", lambda e, bB=bB, q4=q4: e.tensor_copy(QTa[64:72, q4 * 512:(q4 + 1) * 512], banks[bB][64:72, :]),
                              reads=[Rb[bB]], writes=[R_QTa[q4]])
                        S.add("dve", lambda e, bB=bB, q4=q4: e.tensor_copy(QTb[0:8, q4 * 512:(q4 + 1) * 512], banks[bB][0:8, :]),
                              reads=[Rb[bB]], writes=[R_QTb[q4]])

                pend_pv = []
                deferred = []

                def emit_pv(tk):
                    (pslot, w, q0, pvs, first, last, after) = tk
                    for (bank, lhsT, Rl) in pvs:
                        mm(banks[bank][:, q0:q0 + w], lhsT, PT[pslot][:, 0:w], first, last,
                           [R_PT[pslot]] + Rl, [Rb[bank]])
                    if after is not None:
                        after()

                def push(kAP, qAP, Rk, Rq, masks, w, q0, pvs, first, last, after):
                    bS = s_pool.next()
                    nm = len(masks)
                    mm(banks[bS][:, 0:w], kAP, qAP, True, nm == 0, Rk + Rq, [Rb[bS]])
                    for mi, (ml, mr, c0, c1, Rm_) in enumerate(masks):
                        mm(banks[bS][:, c0:c1], ml, mr, False, mi == nm - 1, Rm_, [Rb[bS]])
                    pslot = pt_rot.next()
                    S.add("act", lambda e, bS=bS, pslot=pslot, w=w: e.activation(PT[pslot][:, 0:w], banks[bS][:, 0:w], AF.Exp, scale=0.125),
                          reads=[Rb[bS]], writes=[R_PT[pslot]])
                    pend_pv.append((pslot, w, q0, pvs, first, last, after))
                    while len(pend_pv) > 2:
                        emit_pv(pend_pv.pop(0))

                for qc in range(4):
                    if kind == "diff":
                        h = idx
                        for t in range(2):
                            bO, bN = on_pairs.next()
                            nk = 4 * qc + 4
                            rs = t % 2

                            def fin(t=t, bO=bO, bN=bN, qc=qc, h=h, rs=rs):
                                S.add("act", lambda e: e.activation(lnbuf[rs], banks[bN], AF.Ln), reads=[Rb[bN]], writes=[R_rcp[rs]])
                                S.add("act", lambda e: e.activation(rcp[rs], lnbuf[rs], AF.Exp, scale=-1.0), reads=[R_rcp[rs]], writes=[R_rcp[rs]])
                                if t == 0:
                                    S.add("dve", lambda e: e.tensor_tensor(a1, banks[bO], rcp[rs], ALU.mult),
                                          reads=[Rb[bO], R_rcp[rs]], writes=[R_a1])
                                else:
                                    o2 = qc % 2
                                    S.add("dve", lambda e: e.tensor_tensor(a2, banks[bO], rcp[rs], ALU.mult),
                                          reads=[Rb[bO], R_rcp[rs]], writes=[R_a2])
                                    S.add("dve", lambda e: e.scalar_tensor_tensor(ob[o2], a2, neglam, a1, ALU.mult, ALU.add),
                                          reads=[R_a1, R_a2, R_lam], writes=[R_ob[o2]])
                                    S.add("dve", lambda e: e.tensor_tensor(sqo[o2], ob[o2], ob[o2], ALU.mult), reads=[R_ob[o2]], writes=[R_sqo[o2]])

                                    def rms_tail():
                                        mm(banks[7], ones, sqo[o2], True, True, [R_cst, R_sqo[o2]], [Rb[7]])
                                        S.add("act", lambda e: e.activation(rst[o2], banks[7], AF.Ln, bias=128.0 * EPS, scale=1.0),
                                              reads=[Rb[7]], writes=[R_rst[o2]])
                                        S.add("act", lambda e: e.activation(rst[o2], rst[o2], AF.Exp, scale=-0.5), reads=[R_rst[o2]], writes=[R_rst[o2]])
                                        S.add("dve", lambda e: e.scalar_tensor_tensor(
                                            oT[1][:, h, qc * 512:(qc + 1) * 512], ob[o2], gsc, rst[o2], ALU.mult, ALU.mult),
                                            reads=[R_ob[o2], R_rst[o2], R_lam], writes=[R_oT[1][h][qc]])
                                    deferred.append(rms_tail)

                            Ks, Rk_ = (KTa, R_KTa) if t == 0 else (KTb, R_KTb)
                            for kt in range(nk):
                                j = kt - 4 * qc
                                q0 = 128 * j if j > 0 else 0
                                w = 512 - q0
                                masks = []
                                if j >= 0:
                                    masks.append((ident, tri, 0, 128, [R_cst]))
                                pvs = [(bO, VA[:, kt, :], [R_VA[kt // 4]]), (bN, ones, [R_cst])]
                                first = (kt == 0)
                                last = (kt == nk - 1)
                                after = None
                                if last:
                                    def after(fin=fin):
                                        fin()
                                        while len(deferred) > 1:
                                            deferred.pop(0)()
                                push(Ks[:, kt * 128:(kt + 1) * 128], QTa[:, qc * 512 + q0:qc * 512 + q0 + w],
                                     [Rk_[kt // 4]], [R_QTa[qc]], masks, w, q0, pvs, first, last, after)
                    else:
                        jp = idx
                        bA, bB2 = on_pairs.next()
                        nk = 4 * qc + 4
                        rs = qc % 2

                        def finm(bA=bA, bB2=bB2, qc=qc, jp=jp, rs=rs):
                            S.add("act", lambda e: e.activation(lnbuf[rs][0:64, :], banks[bA][64:128, :], AF.Ln), reads=[Rb[bA]], writes=[R_rcp[rs]])
                            S.add("act", lambda e: e.activation(lnbuf[rs][64:128, :], banks[bB2][0:64, :], AF.Ln), reads=[Rb[bB2]], writes=[R_rcp[rs]])
                            S.add("act", lambda e: e.activation(rcp[rs], lnbuf[rs], AF.Exp, scale=-1.0), reads=[R_rcp[rs]], writes=[R_rcp[rs]])
                            cs = slice(qc * 512, (qc + 1) * 512)
                            S.add("dve", lambda e: e.tensor_tensor(oT[2][0:64, jp, cs], banks[bA][0:64, :], rcp[rs][0:64, :], ALU.mult),
                                  reads=[Rb[bA], R_rcp[rs]], writes=[R_oT[2][jp][qc]])
                            S.add("dve", lambda e: e.tensor_tensor(oT[2][64:128, jp, cs], banks[bB2][64:128, :], rcp[rs][64:128, :], ALU.mult),
                                  reads=[Rb[bB2], R_rcp[rs]], writes=[R_oT[2][jp][qc]])

                        for hl in range(2):
                            Ks, Rk_ = (KTa, R_KTa) if hl == 0 else (KTb, R_KTb)
                            Qs, Rq_ = (QTa, R_QTa) if hl == 0 else (QTb, R_QTb)
                            for kt in range(nk):
                                j = kt - 4 * qc
                                q0 = 128 * j if j > 0 else 0
                                w = 512 - q0
                                masks = []
                                if j >= 0:
                                    masks.append((ident, tri, 0, 128, [R_cst]))
                                if hl == 0:
                                    pvs = [(bA, VA[:, kt, :], [R_VA[kt // 4]])]
                                else:
                                    pvs = [(bB2, VB[:, kt, :], [R_VB[kt // 4]])]
                                first = (kt == 0)
                                last = (kt == nk - 1)
                                push(Ks[:, kt * 128:(kt + 1) * 128], Qs[:, qc * 512 + q0:qc * 512 + q0 + w],
                                     [Rk_[kt // 4]], [Rq_[qc]], masks, w, q0, pvs, first, last,
                                     finm if (last and hl == 1) else None)
                while pend_pv:
                    emit_pv(pend_pv.pop(0))
                while deferred:
                    deferred.pop(0)()
            S.barrier()
            if stop == 2:
                break

            Y = region(Y0)
            mT = Y.alloc((8, S_LEN), BF16)
            R_mT = [[Res() for _ in range(4)] for _ in range(8)]
            wo = Y.alloc((8, D), BF16)
            R_wo = Res()
            ch_wo = new_chan()
            YB = Y.off
            wout = [Y.alloc((4, D), BF16) for _ in range(3)]
            R_wout = Res()
            ch_wout = new_chan()
            wg = [[Y.alloc((8, 128), BF16) for _ in range(3)] for _ in range(2)]
            R_wg = [Res(), Res()]
            ch_wg = [new_chan(), new_chan()]
            gtt = [Y.alloc((512,), F32) for _ in range(3)]
            R_gt = [Res() for _ in range(3)]
            macc = [Y.alloc((512,), F32) for _ in range(2)]
            tmpb = [Y.alloc((512,), F32) for _ in range(2)]
            R_macc = [Res(), Res()]
            R_tmp = [Res(), Res()]
            for br, wd in enumerate((w_co_d, w_do_d, w_mo_d)):
                dma("pool", wout[br], wd[l].rearrange("(k p) n -> p k n", p=128), writes=[R_wout], chan=ch_wout)

            def load_wg(r):
                s = r % 2
                for br in range(3):
                    c0 = 4096 + br * 1024 + r * 128
                    dma("pool", wg[s][br], w_in_l[:, :, c0:c0 + 128], writes=[R_wg[s]], chan=ch_wg[s])

            load_wg(0)
            w_o_l = w_o_d[l].rearrange("(k p) n -> p k n", p=128)
            dma("pool", wo[:, 0:4, :], w_o_l[:, 0:4, :], writes=[R_wo], chan=ch_wo)
            dma("pool", wo[:, 4:8, :], w_o_l[:, 4:8, :], writes=[R_wo], chan=ch_wo)
            all_pool = Rot([0, 1, 2, 3, 4, 5, 6, 7])
            it = 0
            for r in range(8):
                s = r % 2
                if r + 1 < 8:
                    load_wg(r + 1)
                for tc in range(4):
                    m2 = it % 2
                    it += 1
                    for br in range(3):
                        bY = all_pool.next()
                        for k in range(4):
                            mm(banks[bY], wout[br][:, k, r * 128:(r + 1) * 128], oT[br][:, k, tc * 512:(tc + 1) * 512],
                               k == 0, k == 3, [R_wout, R_oT[br][k][tc]], [Rb[bY]])
                        bG = all_pool.next()
                        for k in range(8):
                            mm(banks[bG], wg[s][br][:, k, :], xT[:, k, tc * 512:(tc + 1) * 512], k == 0, k == 7,
                               [R_wg[s]] + R_xT[tc * 4:tc * 4 + 4], [Rb[bG]])
                        S.add("act", lambda e, bG=bG, br=br, r=r: e.activation(
                            gtt[br], banks[bG], AF.Sigmoid, bias=bgate[:, br * 8 + r:br * 8 + r + 1], scale=1.0),
                            reads=[Rb[bG], R_small], writes=[R_gt[br]])
                        if br == 0:
                            S.add("dve", lambda e, bY=bY, m2=m2: e.tensor_tensor(macc[m2], banks[bY], gtt[0], ALU.mult),
                                  reads=[Rb[bY], R_gt[0]], writes=[R_macc[m2]])
                        elif br == 1:
                            S.add("dve", lambda e, bY=bY, m2=m2: e.tensor_tensor(tmpb[m2], banks[bY], gtt[1], ALU.mult),
                                  reads=[Rb[bY], R_gt[1]], writes=[R_tmp[m2]])
                            S.add("pool", lambda e, m2=m2: e.tensor_tensor(macc[m2], macc[m2], tmpb[m2], ALU.add),
                                  reads=[R_macc[m2], R_tmp[m2]], writes=[R_macc[m2]])
                        else:
                            S.add("dve", lambda e, bY=bY, m2=m2: e.tensor_tensor(tmpb[m2], banks[bY], gtt[2], ALU.mult),
                                  reads=[Rb[bY], R_gt[2]], writes=[R_tmp[m2]])
                            S.add("dve", lambda e, m2=m2, r=r, tc=tc: e.tensor_tensor(
                                mT[:, r, tc * 512:(tc + 1) * 512], macc[m2], tmpb[m2], ALU.add),
                                reads=[R_macc[m2], R_tmp[m2]], writes=[R_mT[r][tc]])
            S.barrier()
            if stop == 3:
                break

            Y = region(YB)
            lng = Y.alloc((D,), F32)
            lnb = Y.alloc((D,), F32)
            R_ln = Res()
            ch_ln = new_chan()
            W1 = [Y.alloc((8, 512), BF16) for _ in range(2)]
            R_W1 = [Res(), Res()]
            ch_W1 = [new_chan(), new_chan()]
            W2 = [Y.alloc((4, D), BF16) for _ in range(2)]
            R_W2 = [Res(), Res()]
            ch_W2 = [new_chan(), new_chan()]
            xbs = [Y.alloc((D,), BF16) for _ in range(2)]
            R_xbs = [Res(), Res()]
            xin = [Y.alloc((D,), F32) for _ in range(2)]
            R_xin = [Res(), Res()]
            ch_xin = [new_chan(), new_chan()]
            dma("sp", lng, ln_d[l, 0].partition_broadcast(128), writes=[R_ln], chan=ch_ln)
            dma("sp", lnb, ln_d[l, 1].partition_broadcast(128), writes=[R_ln], chan=ch_ln)
            w1_l = w_f1_d[l].rearrange("(k p) n -> p k n", p=128)
            w2_l = w_f2_d[l].rearrange("(k p) n -> p k n", p=128)

            def load_w1(i):
                s = i % 2
                dma("pool", W1[s], w1_l[:, :, i * 512:(i + 1) * 512], writes=[R_W1[s]], chan=ch_W1[s])

            def load_w2(i):
                s = i % 2
                dma("pool", W2[s], w2_l[:, i * 4:(i + 1) * 4, :], writes=[R_W2[s]], chan=ch_W2[s])

            load_w1(0)
            load_w1(1)
            load_w2(0)
            load_w2(1)
            xsrc = x_d if l == 0 else y_d
            deferred = []
            for tt in range(NT):
                s = tt % 2
                dma("sp", xin[s], xsrc[tt * 128:(tt + 1) * 128, :], reads=[R_y[tt]], writes=[R_xin[s]], chan=ch_xin[s])
                for half in range(2):
                    bW = proj_pool.next()
                    for k in range(8):
                        mm(banks[bW], mT[:, k, tt * 128:(tt + 1) * 128], wo[:, k, half * 512:(half + 1) * 512], k == 0, k == 7,
                           [R_mT[k][tt // 4], R_wo], [Rb[bW]])
                    S.add("dve", lambda e, bW=bW, tt=tt, half=half, s=s: e.scalar_tensor_tensor(
                        x_res[:, tt, half * 512:(half + 1) * 512], xin[s][:, half * 512:(half + 1) * 512], ALPHA, banks[bW],
                        ALU.mult, ALU.add), reads=[Rb[bW], R_xin[s]], writes=[R_xres[tt]])
                old = list(deferred)
                layer_norm_tile(tt, lng, lnb, R_ln, False, True, xbs, R_xbs, deferred)
                run_pipe(old)
                deferred[:] = [it for it in deferred if it]
            run_pipe(deferred, drain=True)
            S.barrier()
            if stop == 4:
                break

            Y = region(Y0)
            hTg = Y.alloc((8, S_LEN), BF16)
            R_hTg = [[Res() for _ in range(4)] for _ in range(8)]
            rl = [Y.alloc((512,), F32) for _ in range(2)]
            R_rl = [Res(), Res()]
            assert Y.off <= YB
            dma("sp", lng, ln_d[l, 2].partition_broadcast(128), writes=[R_ln], chan=ch_ln)
            dma("sp", lnb, ln_d[l, 3].partition_broadcast(128), writes=[R_ln], chan=ch_ln)
            deferred = []
            it = 0
            for g in range(4):
                for hf in range(2):
                    i = g * 2 + hf
                    s = i % 2
                    for fc4 in range(4):
                        fc = hf * 4 + fc4
                        for tc in range(4):
                            bH = proj_pool.next() if False else all_pool.next()
                            for k in range(8):
                                mm(banks[bH], W1[s][:, k, fc4 * 128:(fc4 + 1) * 128], xT[:, k, tc * 512:(tc + 1) * 512],
                                   k == 0, k == 7, [R_W1[s]] + R_xT[tc * 4:tc * 4 + 4], [Rb[bH]])
                            r2 = it % 2
                            it += 1
                            bcol = bff1[:, g * 8 + fc:g * 8 + fc + 1]
                            S.add("act", lambda e, bH=bH, r2=r2, bcol=bcol: e.activation(rl[r2], banks[bH], AF.Relu, bias=bcol, scale=1.0),
                                  reads=[Rb[bH], R_small], writes=[R_rl[r2]])
                            S.add("dve", lambda e, bH=bH, r2=r2, bcol=bcol, fc=fc, tc=tc: e.scalar_tensor_tensor(
                                hTg[:, fc, tc * 512:(tc + 1) * 512], banks[bH], bcol, rl[r2], ALU.add, ALU.mult),
                                reads=[Rb[bH], R_rl[r2], R_small], writes=[R_hTg[fc][tc]])
                    if i + 2 < 8:
                        load_w1(i + 2)
                for tt in range(NT):
                    for half in range(2):
                        bF = all_pool.next()
                        if g == 0:
                            mm(banks[bF], ones[0:1, :], b2row[0:1, half * 512:(half + 1) * 512], True, False,
                               [R_cst, R_b2], [Rb[bF]])
                        for fc in range(8):
                            mm(banks[bF], hTg[:, fc, tt * 128:(tt + 1) * 128], W2[fc // 4][:, fc % 4, half * 512:(half + 1) * 512],
                               (fc == 0 and g != 0), fc == 7, [R_hTg[fc][tt // 4], R_W2[fc // 4]], [Rb[bF]])
                        xr = x_res[:, tt, half * 512:(half + 1) * 512]
                        if g == 0:
                            S.add("dve", lambda e, bF=bF, xr=xr: e.scalar_tensor_tensor(xr, xr, ALPHA, banks[bF], ALU.mult, ALU.add),
                                  reads=[Rb[bF], R_xres[tt]], writes=[R_xres[tt]])
                        else:
                            S.add("dve", lambda e, bF=bF, xr=xr: e.tensor_tensor(xr, xr, banks[bF], ALU.add),
                                  reads=[Rb[bF], R_xres[tt]], writes=[R_xres[tt]])
                    if g == 3:
                        old = list(deferred)
                        layer_norm_tile(tt, lng, lnb, R_ln, True, not last_layer, xbs, R_xbs, deferred)
                        run_pipe(old)
                        deferred[:] = [it for it in deferred if it]
                if g < 3:
                    load_w2(2 * (g + 1))
                    load_w2(2 * (g + 1) + 1)
            run_pipe(deferred, drain=True)
            S.barrier()

        S.finalize()
        with nc.Block() as block:
            @block.tensor
            def _(e):
                S.emit("pe", e, sems)

            @block.scalar
            def _(e):
                S.emit("act", e, sems)

            @block.vector
            def _(e):
                S.emit("dve", e, sems)

            @block.gpsimd
            def _(e):
                S.emit("pool", e, sems)

            @block.sync
            def _(e):
                S.emit("sp", e, sems, final_waits=[ch_y])
    return nc


def _constants():
    pos = np.arange(S_LEN, dtype=np.float32)
    inv = (np.float32(10000.0) ** (-np.arange(0, 64, 2, dtype=np.float32) / np.float32(64))).astype(np.float32)
    ang = (pos[:, None] * inv[None, :]).astype(np.float32)
    ang = np.concatenate([ang, ang], axis=-1)
    cos = np.cos(ang).astype(np.float32).T
    sin = np.sin(ang).astype(np.float32).T
    sgn = np.where(np.arange(64) < 32, -1.0, 1.0).astype(np.float32)[:, None]
    cos_t = np.concatenate([cos, cos], axis=0)
    sin_t = np.concatenate([sin * sgn, sin * sgn], axis=0)
    ident = np.eye(128, dtype=np.float32)
    Rm = np.zeros((128, 128), np.float32)
    for j in range(128):
        b, o = divmod(j, 64)
        Rm[b * 64 + (o + 32) % 64, j] = 1.0
    k = np.arange(128)[:, None]
    q = np.arange(128)[None, :]
    tri = np.where(k > q, MASKV, 0.0).astype(np.float32)
    ones = np.ones((128, 128), np.float32)
    onesE = np.zeros((128, 128), np.float32)
    onesE[:, :64] = 1.0
    onesO = np.zeros((128, 128), np.float32)
    onesO[:, 64:] = 1.0
    cst = np.concatenate([ident, Rm, tri, ones, onesE, onesO], axis=1)
    ind = np.zeros((8, S_LEN), np.float32)
    blk = np.arange(S_LEN) // 256
    for n in range(8):
        ind[n, blk == n] = 1.0
    own = (np.arange(16) // 2)[:, None]
    n = np.arange(8)[None, :]
    vneg = np.where(n < own, 0.0, -1.0e30).astype(np.float32)
    vm = np.where(n < own, MASKV, 0.0).astype(np.float32)
    vmask = np.concatenate([vneg.reshape(-1), vm.reshape(-1)])[None, :].repeat(128, axis=0)
    return dict(cos_t=np.ascontiguousarray(cos_t), sin_t=np.ascontiguousarray(sin_t), cst=np.ascontiguousarray(cst),
                ind_h=ind, vmask=np.ascontiguousarray(vmask.astype(np.float32)))


def _prep_inputs(inp):
    f = lambda a: np.ascontiguousarray(np.asarray(a, dtype=np.float32))
    L = DEPTH
    shared = {}
    for k in ("w_in", "w_conv_out", "w_diff_out", "w_moba_out", "w_o", "w_ff1", "w_ff2", "b_ff2"):
        shared[k] = f(inp[k])
    shared["b_gate_t"] = f(np.asarray(inp["b_gate"]).reshape(L, 24, 128).transpose(0, 2, 1))
    cw = np.asarray(inp["conv_w"]).reshape(L, 31, 4, 128)
    shared["conv_w_t"] = f(cw.transpose(0, 3, 2, 1).reshape(L, 128, 4 * 31))
    cp = np.stack([np.asarray(inp["conv_b"]), np.asarray(inp["conv_ln_g"]), np.asarray(inp["conv_ln_b"])], axis=1)
    shared["conv_p_t"] = f(cp.reshape(L, 3, 4, 128).transpose(0, 3, 1, 2).reshape(L, 128, 12))
    shared["lam_all"] = f(np.concatenate([np.asarray(inp[k]) for k in ("lam_q1", "lam_k1", "lam_q2", "lam_k2")], axis=1))
    shared["subln_g_t"] = f(np.asarray(inp["diff_subln_g"]).reshape(L, 128, 1))
    shared["ln_all"] = f(np.stack([np.asarray(inp[k]) for k in ("ln1_g", "ln1_b", "ln2_g", "ln2_b")], axis=1))
    shared["b_ff1_t"] = f(np.asarray(inp["b_ff1"]).reshape(L, 32, 128).transpose(0, 2, 1))
    shared.update(_constants())
    return shared


_NC_CACHE = {}


def kernel(**inputs):
    x = np.ascontiguousarray(np.asarray(inputs["x"], dtype=np.float32))
    shared = _prep_inputs(inputs)
    if "nc" not in _NC_CACHE:
        _NC_CACHE["nc"] = build_program(DEPTH)
    nc = _NC_CACHE["nc"]
    in_maps = []
    for b in range(8):
        m = dict(shared)
        m["x"] = x[b]
        in_maps.append(m)
    res = run_bass_kernel_spmd(nc, in_maps, core_ids=list(range(8)))
    return np.stack([np.asarray(r["y"], dtype=np.float32) for r in res.results], axis=0)
```

```python
import math
import os
from contextlib import ExitStack
import numpy as np
import concourse.bass as bass
import concourse.mybir as mybir
from concourse.bass_utils import run_bass_kernel_spmd

F32 = mybir.dt.float32
BF16 = mybir.dt.bfloat16
AF = mybir.ActivationFunctionType
ALU = mybir.AluOpType
AX = mybir.AxisListType

COMPUTE = ("pe", "act", "dve", "pool")
ENGS = ("pe", "act", "dve", "pool", "sp")

S_LEN = 2048
D = 1024
DEPTH = 4
NT = 16
EPS = 1e-5
ALPHA = (2.0 * DEPTH) ** 0.25
MASKV = -30000.0
ARENA_BYTES = 212800


class Res:
    __slots__ = ("w", "r", "rd", "excl")

    def __init__(self, excl=False):
        self.w = None
        self.r = {}
        self.rd = []
        self.excl = excl


class Chan:
    __slots__ = ("sem", "count")

    def __init__(self, sem):
        self.sem = sem
        self.count = 0


class Op:
    __slots__ = ("eng", "fn", "cdeps", "ddeps", "idx", "signal", "seq", "chan", "is_dma")

    def __init__(self, eng, fn):
        self.eng = eng
        self.fn = fn
        self.cdeps = {}
        self.ddeps = {}
        self.signal = False
        self.seq = 0
        self.chan = None
        self.is_dma = False


class Sched:
    def __init__(self):
        self.ops = {e: [] for e in ENGS}
        self.pending_barrier = {e: None for e in ENGS}
        self.pending_chans = {e: None for e in ENGS}
        self.chans = []

    def _dep_on(self, op, d):
        if d is None or d is op:
            return
        if d.is_dma:
            ch = d.chan
            if op.is_dma and op.chan is ch:
                return
            op.ddeps[ch] = max(op.ddeps.get(ch, 0), ch.count)
        else:
            if d.eng == "pe" and op.eng == "pe" and not op.is_dma:
                return
            cur = op.cdeps.get(d.eng)
            if cur is None or cur.idx < d.idx:
                op.cdeps[d.eng] = d

    def add(self, eng, fn, reads=(), writes=(), chan=None):
        op = Op(eng, fn)
        op.idx = len(self.ops[eng])
        if chan is not None:
            op.is_dma = True
            op.chan = chan
        ex = [r for r in reads if r.excl]
        if ex:
            reads = [r for r in reads if not r.excl]
            writes = list(writes) + ex
        for r in reads:
            self._dep_on(op, r.w)
        for w in writes:
            self._dep_on(op, w.w)
            for d in w.r.values():
                self._dep_on(op, d)
            for d in w.rd:
                self._dep_on(op, d)
        pb = self.pending_barrier[eng]
        if pb is not None:
            for d in pb:
                self._dep_on(op, d)
            self.pending_barrier[eng] = None
            for ch, cnt in self.pending_chans[eng]:
                if cnt > 0 and not (op.is_dma and op.chan is ch):
                    op.ddeps[ch] = max(op.ddeps.get(ch, 0), cnt)
            self.pending_chans[eng] = None
        if chan is not None:
            chan.count += 16
        for r in reads:
            if op.is_dma:
                r.rd.append(op)
            else:
                r.r[eng] = op
        for w in writes:
            w.w = op
            w.r = {}
            w.rd = []
        self.ops[eng].append(op)
        return op

    def barrier(self):
        last = []
        for e in ENGS:
            if self.ops[e]:
                last.append(self.ops[e][-1])
        snap = [(ch, ch.count) for ch in self.chans]
        for e in ENGS:
            self.pending_barrier[e] = list(last)
            self.pending_chans[e] = snap

    def finalize(self):
        for e in ENGS:
            for op in self.ops[e]:
                for d in op.cdeps.values():
                    d.signal = True
        for e in ENGS:
            n = 0
            for op in self.ops[e]:
                if op.signal and not op.is_dma:
                    n += 1
                    op.seq = n

    def emit(self, eng, engine_obj, sems, final_waits=()):
        waited = {}
        for op in self.ops[eng]:
            for d in op.cdeps.values():
                s = sems[d.eng]
                if waited.get(id(s), 0) < d.seq:
                    engine_obj.wait_ge(s, d.seq)
                    waited[id(s)] = d.seq
            for ch, cnt in op.ddeps.items():
                if waited.get(id(ch.sem), 0) < cnt:
                    engine_obj.wait_ge(ch.sem, cnt)
                    waited[id(ch.sem)] = cnt
            ins = op.fn(engine_obj)
            if op.is_dma:
                ins.then_inc(op.chan.sem, 16)
            elif op.signal:
                ins.then_inc(sems[eng], 1)
        for ch in final_waits:
            if ch.count > 0 and waited.get(id(ch.sem), 0) < ch.count:
                engine_obj.wait_ge(ch.sem, ch.count)


class Arena:
    def __init__(self, tensor, nbytes):
        self.t = tensor
        self.nbytes = nbytes
        self.off = 0

    def alloc(self, shape, dtype, parts=128):
        esz = 2 if dtype == BF16 else 4
        n = 1
        for s in shape:
            n *= s
        nb = (n * esz + 31) // 32 * 32
        a = self.off
        self.off += nb
        assert self.off <= self.nbytes, ("arena overflow", self.off, self.nbytes)
        ap = self.t[0:parts, a // 4:(a + nb) // 4]
        if dtype != F32:
            ap = ap.bitcast(dtype)
        ap = ap[:, 0:n]
        if len(shape) == 2:
            ap = ap.rearrange("p (a b) -> p a b", b=shape[1])
        elif len(shape) == 3:
            ap = ap.rearrange("p (a b c) -> p a b c", b=shape[1], c=shape[2])
        return ap


class Rot:
    def __init__(self, items):
        self.items = items
        self.i = 0

    def next(self):
        v = self.items[self.i % len(self.items)]
        self.i += 1
        return v


def build_program(n_layers=DEPTH, stop=99):
    nc = bass.Bass("TRN2", target_bir_lowering=False)

    def din(name, shape):
        return nc.dram_tensor(name, list(shape), F32, kind="ExternalInput").ap()

    x_d = din("x", [S_LEN, D])
    w_in_d = din("w_in", [DEPTH, D, 7168])
    w_co_d = din("w_conv_out", [DEPTH, 512, D])
    w_do_d = din("w_diff_out", [DEPTH, 512, D])
    w_mo_d = din("w_moba_out", [DEPTH, 512, D])
    w_o_d = din("w_o", [DEPTH, D, D])
    w_f1_d = din("w_ff1", [DEPTH, D, 4096])
    w_f2_d = din("w_ff2", [DEPTH, 4096, D])
    bgate_d = din("b_gate_t", [DEPTH, 128, 24])
    convw_d = din("conv_w_t", [DEPTH, 128, 4 * 31])
    convp_d = din("conv_p_t", [DEPTH, 128, 12])
    lam_d = din("lam_all", [DEPTH, 256])
    subg_d = din("subln_g_t", [DEPTH, 128, 1])
    ln_d = din("ln_all", [DEPTH, 4, D])
    bff1_d = din("b_ff1_t", [DEPTH, 128, 32])
    bff2_d = din("b_ff2", [DEPTH, D])
    cos_d = din("cos_t", [128, S_LEN])
    sin_d = din("sin_t", [128, S_LEN])
    cst_d = din("cst", [128, 6 * 128])
    ind_d = din("ind_h", [8, S_LEN])
    vmask_d = din("vmask", [128, 256])
    y_d = nc.dram_tensor("y", [S_LEN, D], F32, kind="ExternalOutput").ap()

    es = ExitStack()
    with es:
        arena_t = es.enter_context(nc.sbuf_tensor("arena", [128, ARENA_BYTES // 4], F32))
        banks_t = [es.enter_context(nc.psum_tensor(f"ps{i}", [128, 512], F32)) for i in range(8)]
        sems = {e: es.enter_context(nc.semaphore(f"s_{e}")) for e in ENGS}
        _nch = [0]

        def new_chan():
            _nch[0] += 1
            c = Chan(es.enter_context(nc.semaphore(f"ch{_nch[0]}")))
            S.chans.append(c)
            return c

        S = Sched()
        banks = [b[:, :] for b in banks_t]
        Rb = [Res(excl=True) for _ in range(8)]

        P = Arena(arena_t, ARENA_BYTES)
        xT = P.alloc((8, S_LEN), BF16)
        R_xT = [Res() for _ in range(NT)]
        cst = P.alloc((6, 128), BF16)
        ident, Rm, tri, ones, onesE, onesO = [cst[:, i, :] for i in range(6)]
        R_cst = Res()
        vmask = P.alloc((2, 16, 8), F32)
        biaspad = P.alloc((16, 72), BF16)
        R_biaspad = Res()
        bgate = P.alloc((24,), F32)
        convw = P.alloc((4, 31), F32)
        convp = P.alloc((3, 4), F32)
        lamt = P.alloc((4, 64), F32)
        subg = P.alloc((1,), F32)
        bff1 = P.alloc((32,), F32)
        b2row = P.alloc((D,), BF16)
        lamp = P.alloc((2, 64), F32)
        lams = P.alloc((4,), F32)
        neglam = P.alloc((1,), F32)
        gsc = P.alloc((1,), F32)
        R_small = Res()
        R_lam = Res()
        NST = 4
        stt_l = [P.alloc((12,), F32) for _ in range(NST)]
        mv_l = [P.alloc((2,), F32) for _ in range(NST)]
        rstd1_l = [P.alloc((1,), F32) for _ in range(NST)]
        nmr1_l = [P.alloc((1,), F32) for _ in range(NST)]
        R_st_l = [Res() for _ in range(NST)]
        X0 = P.off
        XBYTES = 65536
        Y0 = X0 + XBYTES
        ch_small = new_chan()
        ch_y = new_chan()
        ch_misc = new_chan()
        ch_misc_sp = new_chan()
        ch_small_pool = new_chan()
        ch_trig = new_chan()
        R_b2 = Res()
        R_y = [Res() for _ in range(NT)]

        def region(off):
            a = Arena(arena_t, ARENA_BYTES)
            a.off = off
            return a

        XA = region(X0)
        oT = [XA.alloc((4, S_LEN), BF16) for _ in range(3)]
        cosT = XA.alloc((S_LEN,), F32)
        sinT = XA.alloc((S_LEN,), F32)
        assert XA.off <= Y0
        R_oT = [[[Res() for _ in range(4)] for _ in range(4)] for _ in range(3)]
        R_trig = Res()
        XB = region(X0)
        x_res = XB.alloc((NT, D), F32)
        R_xres = [Res() for _ in range(NT)]
        XC = region(X0 + 16384)
        v32 = XC.alloc((4, S_LEN), F32)
        R_v32 = [[Res() for _ in range(4)] for _ in range(4)]

        def dma(q, out, in_, reads=(), writes=(), chan=None):
            S.add(q, lambda e: e.dma_start(out=out, in_=in_), reads=reads, writes=writes, chan=chan)

        def mm(out, lhsT, rhs, start, stop, reads, writes):
            S.add("pe", lambda e: e.matmul(out, lhsT, rhs, start=start, stop=stop), reads=reads, writes=writes)

        dma("pool", cst.rearrange("p a b -> p (a b)"), cst_d, writes=[R_cst], chan=ch_misc)
        dma("sp", vmask.rearrange("p a b c -> p (a b c)"), vmask_d, writes=[R_cst], chan=ch_misc_sp)
        S.add("pool", lambda e: e.memset(biaspad, 0.0), writes=[R_biaspad])

        proj_pool = Rot([0, 1, 2])
        aux_pool = Rot([3, 4, 5, 6])

        def make_xT(tt, xb_ap, R_xb):
            b = aux_pool.next()
            psT = banks[b].bitcast(BF16)
            for k in range(8):
                S.add("pe", lambda e, k=k: e.transpose(psT[:, k * 128:(k + 1) * 128], xb_ap[:, k * 128:(k + 1) * 128], ident),
                      reads=[R_xb, R_cst], writes=[Rb[b]])
            if os.environ.get("DBG_XT"):
                S.add("dve", lambda e: e.tensor_copy(xT[:, :, tt * 128:(tt + 1) * 128],
                                                     psT.rearrange("p (k t) -> p k t", t=128)),
                      reads=[Rb[b]], writes=[R_xT[tt]])
            else:
                S.add("act", lambda e: e.copy(xT[:, :, tt * 128:(tt + 1) * 128],
                                              psT.rearrange("p (k t) -> p k t", t=128)),
                      reads=[Rb[b]], writes=[R_xT[tt]])

        Y = region(Y0)
        xb0 = [Y.alloc((D,), BF16) for _ in range(2)]
        R_xb0 = [Res(), Res()]
        ch_xb0 = [new_chan(), new_chan()]
        for tt in range(NT):
            s = tt % 2
            dma("pool", xb0[s], x_d[tt * 128:(tt + 1) * 128, :], writes=[R_xb0[s]], chan=ch_xb0[s])
            make_xT(tt, xb0[s], R_xb0[s])
        S.barrier()

        def layer_norm_tile(tt, lng, lnb, R_ln, write_out, make_next, xbs, R_xbs, pipe):
            xr = x_res[:, tt, :]
            R = R_xres[tt]
            stt, mv, rstd1, nmr1, R_st = stt_l[tt % NST], mv_l[tt % NST], rstd1_l[tt % NST], nmr1_l[tt % NST], R_st_l[tt % NST]
            S.add("dve", lambda e: e.bn_stats(stt[:, 0:6], xr[:, 0:512]), reads=[R], writes=[R_st])
            S.add("dve", lambda e: e.bn_stats(stt[:, 6:12], xr[:, 512:1024]), reads=[R], writes=[R_st])
            S.add("dve", lambda e: e.bn_aggr(mv, stt), reads=[R_st], writes=[R_st])
            S.add("act", lambda e: e.activation(rstd1, mv[:, 1:2], AF.Ln, bias=EPS, scale=1.0), reads=[R_st], writes=[R_st])
            S.add("act", lambda e: e.activation(rstd1, rstd1, AF.Exp, scale=-0.5), reads=[R_st], writes=[R_st])
            S.add("dve", lambda e: e.scalar_tensor_tensor(nmr1, mv[:, 0:1], -1.0, rstd1, ALU.mult, ALU.mult),
                  reads=[R_st], writes=[R_st])
            S.add("act", lambda e: e.activation(xr, xr, AF.Identity, bias=nmr1, scale=rstd1), reads=[R_st, R], writes=[R])
            s = tt % 2

            def stage_b():
                S.add("pool", lambda e: e.tensor_tensor(xr, xr, lng, ALU.mult), reads=[R, R_ln], writes=[R])
                S.add("dve", lambda e: e.tensor_tensor(xr, xr, lnb, ALU.add), reads=[R, R_ln], writes=[R])
                if write_out:
                    dma("sp", y_d[tt * 128:(tt + 1) * 128, :], xr, reads=[R], writes=[R_y[tt]], chan=ch_y)
                if make_next:
                    S.add("act", lambda e: e.copy(xbs[s], xr), reads=[R], writes=[R_xbs[s]])

            def stage_c():
                if make_next:
                    make_xT(tt, xbs[s], R_xbs[s])

            pipe.append([stage_b, stage_c])

        def run_pipe(pipe, drain=False):
            while True:
                for item in list(pipe):
                    item.pop(0)()
                    if not item:
                        pipe.remove(item)
                if not drain or not pipe:
                    break

        for l in range(n_layers):
            if stop == 0:
                break
            lam_init = 0.8 - 0.6 * math.exp(-0.3 * l)
            last_layer = (l == n_layers - 1)
            w_in_l = w_in_d[l].rearrange("(k p) n -> p k n", p=128)

            dma("sp", bgate, bgate_d[l], writes=[R_small], chan=ch_small)
            dma("sp", convw.rearrange("p a b -> p (a b)"), convw_d[l], writes=[R_small], chan=ch_small)
            dma("sp", convp.rearrange("p a b -> p (a b)"), convp_d[l], writes=[R_small], chan=ch_small)
            dma("sp", lamt.rearrange("p a b -> p (a b)"), lam_d[l].partition_broadcast(128), writes=[R_small], chan=ch_small)
            dma("sp", subg, subg_d[l], writes=[R_small], chan=ch_small)
            dma("sp", bff1, bff1_d[l], writes=[R_small], chan=ch_small)
            dma("pool", b2row[0:1, :], bff2_d[l:l + 1, :], writes=[R_b2], chan=ch_small_pool)
            dma("sp", cosT, cos_d, writes=[R_trig], chan=ch_trig)
            dma("sp", sinT, sin_d, writes=[R_trig], chan=ch_trig)
            S.add("dve", lambda e: e.tensor_tensor(lamp[:, 0, :], lamt[:, 0, :], lamt[:, 1, :], ALU.mult), reads=[R_small], writes=[R_lam])
            S.add("dve", lambda e: e.tensor_tensor(lamp[:, 1, :], lamt[:, 2, :], lamt[:, 3, :], ALU.mult), reads=[R_small], writes=[R_lam])
            S.add("dve", lambda e: e.tensor_reduce(lams[:, 0:2], lamp, AX.X, ALU.add), reads=[R_lam], writes=[R_lam])
            S.add("act", lambda e: e.activation(lams[:, 2:4], lams[:, 0:2], AF.Exp), reads=[R_lam], writes=[R_lam])
            S.add("dve", lambda e: e.tensor_tensor(neglam, lams[:, 3:4], lams[:, 2:3], ALU.subtract), reads=[R_lam], writes=[R_lam])
            S.add("dve", lambda e, li=lam_init: e.tensor_scalar(neglam, neglam, -li, None, ALU.add), reads=[R_lam], writes=[R_lam])
            S.add("dve", lambda e, li=lam_init: e.tensor_scalar(gsc, subg, (1.0 - li) * math.sqrt(128.0), None, ALU.mult),
                  reads=[R_small], writes=[R_lam])

            Y = region(Y0)
            wA = [Y.alloc((8, 128), BF16) for _ in range(2)]
            wG = [Y.alloc((8, 128), BF16) for _ in range(2)]
            R_wc = [Res(), Res()]
            ch_wc = [new_chan(), new_chan()]
            hT = [Y.alloc((2080,), BF16) for _ in range(2)]
            R_hT = [[Res() for _ in range(5)] for _ in range(2)]
            diag2 = [Y.alloc((31, 128), BF16) for _ in range(2)]
            R_diag2 = [Res(), Res()]
            sg = [Y.alloc((512,), F32) for _ in range(2)]
            R_sg = [Res(), Res()]
            YT0 = ARENA_BYTES - 18432
            YT = region(YT0)
            vb = [YT.alloc((512,), BF16) for _ in range(4)]
            sq = [YT.alloc((512,), BF16) for _ in range(4)]
            R_vb = [Res() for _ in range(4)]
            R_sq = [Res() for _ in range(4)]
            mean = YT.alloc((512,), F32)
            msq = YT.alloc((512,), F32)
            var = YT.alloc((512,), F32)
            zt = [YT.alloc((512,), F32) for _ in range(2)]
            assert Y.off <= YT0
            R_mean, R_var = Res(), Res()
            R_zt = [Res(), Res()]

            for s in range(2):
                S.add("pool", lambda e, s=s: e.memset(hT[s][:, 0:30], 0.0), writes=[R_hT[s][0]])
            for cr in range(4):
                s = cr % 2
                dma("pool", wA[s], w_in_l[:, :, cr * 128:(cr + 1) * 128], writes=[R_wc[s]], chan=ch_wc[s])
                dma("pool", wG[s], w_in_l[:, :, 512 + cr * 128:512 + (cr + 1) * 128], writes=[R_wc[s]], chan=ch_wc[s])
                diag = diag2[s]
                R_diag = R_diag2[s]
                S.add("dve", lambda e, cr=cr, diag=diag: e.tensor_tensor(
                    diag, ident.rearrange("p (o c) -> p o c", o=1).broadcast_to([128, 31, 128]),
                    convw[:, cr, :].rearrange("p (j o) -> p j o", o=1).broadcast_to([128, 31, 128]), ALU.mult),
                    reads=[R_cst, R_small], writes=[R_diag])
                for tc in range(4):
                    bA = proj_pool.next()
                    for k in range(8):
                        mm(banks[bA], wA[s][:, k, :], xT[:, k, tc * 512:(tc + 1) * 512], k == 0, k == 7,
                           [R_wc[s]] + R_xT[tc * 4:tc * 4 + 4], [Rb[bA]])
                    bG = proj_pool.next()
                    for k in range(8):
                        mm(banks[bG], wG[s][:, k, :], xT[:, k, tc * 512:(tc + 1) * 512], k == 0, k == 7,
                           [R_wc[s]] + R_xT[tc * 4:tc * 4 + 4], [Rb[bG]])
                    g2 = tc % 2
                    S.add("act", lambda e, bG=bG, g2=g2: e.activation(sg[g2], banks[bG], AF.Sigmoid), reads=[Rb[bG]], writes=[R_sg[g2]])
                    S.add("dve", lambda e, bA=bA, g2=g2, s=s, tc=tc: e.tensor_tensor(
                        hT[s][:, 30 + tc * 512:30 + (tc + 1) * 512], banks[bA], sg[g2], ALU.mult),
                        reads=[Rb[bA], R_sg[g2]], writes=[R_hT[s][tc + 1]])
                for tc in range(4):
                    bC = aux_pool.next()
                    for j in range(31):
                        mm(banks[bC], diag[:, j, :], hT[s][:, tc * 512 + j:tc * 512 + j + 512], j == 0, j == 30,
                           [R_diag, R_hT[s][tc], R_hT[s][tc + 1]], [Rb[bC]])
                    S.add("act", lambda e, bC=bC, cr=cr, tc=tc: e.activation(
                        v32[:, cr, tc * 512:(tc + 1) * 512], banks[bC], AF.Identity, bias=convp[:, 0, cr:cr + 1], scale=1.0),
                        reads=[Rb[bC], R_small], writes=[R_v32[cr][tc]])
            R_alias = [R_oT[b][k][t] for b in (1, 2) for k in range(4) for t in range(4)]

            def convln_p1(tc):
                for cr in range(4):
                    src = v32[:, cr, tc * 512:(tc + 1) * 512]
                    S.add("act", lambda e, cr=cr, src=src: e.copy(vb[cr], src), reads=[R_v32[cr][tc]] + R_alias, writes=[R_vb[cr]])
                    S.add("act", lambda e, cr=cr, src=src: e.activation(sq[cr], src, AF.Square), reads=[R_v32[cr][tc]] + R_alias, writes=[R_sq[cr]])

            def convln_p2(tc):
                bM = aux_pool.next()
                for cr in range(4):
                    mm(banks[bM], ones, vb[cr], cr == 0, cr == 3, [R_cst, R_vb[cr]], [Rb[bM]])
                bQ = aux_pool.next()
                for cr in range(4):
                    mm(banks[bQ], ones, sq[cr], cr == 0, cr == 3, [R_cst, R_sq[cr]], [Rb[bQ]])
                S.add("dve", lambda e: e.tensor_scalar(mean, banks[bM], 1.0 / 512, None, ALU.mult), reads=[Rb[bM]], writes=[R_mean])
                S.add("dve", lambda e: e.tensor_tensor(msq, mean, mean, ALU.mult), reads=[R_mean], writes=[R_var])
                S.add("dve", lambda e: e.scalar_tensor_tensor(var, banks[bQ], 1.0 / 512, msq, ALU.mult, ALU.subtract),
                      reads=[Rb[bQ], R_var], writes=[R_var])
                S.add("act", lambda e: e.activation(var, var, AF.Ln, bias=EPS, scale=1.0), reads=[R_var], writes=[R_var])
                S.add("act", lambda e: e.activation(var, var, AF.Exp, scale=-0.5), reads=[R_var], writes=[R_var])
                for cr in range(4):
                    z = zt[cr % 2]
                    Rz = R_zt[cr % 2]
                    src = v32[:, cr, tc * 512:(tc + 1) * 512]
                    S.add("dve", lambda e, z=z, src=src: e.tensor_tensor(z, src, mean, ALU.subtract), reads=[R_v32[cr][tc], R_mean] + R_alias, writes=[Rz])
                    S.add("dve", lambda e, z=z: e.tensor_tensor(z, z, var, ALU.mult), reads=[Rz, R_var], writes=[Rz])
                    S.add("act", lambda e, z=z, cr=cr: e.activation(
                        oT[0][:, cr, tc * 512:(tc + 1) * 512], z, AF.Silu, bias=convp[:, 2, cr:cr + 1], scale=convp[:, 1, cr:cr + 1]),
                        reads=[Rz, R_small], writes=[R_oT[0][cr][tc]])
            S.barrier()
            if stop == 1:
                break

            Y = region(Y0)
            wq = [Y.alloc((8, 128), BF16) for _ in range(2)]
            wk = [Y.alloc((8, 128), BF16) for _ in range(2)]
            wv = [Y.alloc((8, 128), BF16) for _ in range(2)]
            R_wj = [Res(), Res()]
            ch_wj = [new_chan(), new_chan()]
            QTa = Y.alloc((S_LEN,), BF16)
            QTb = Y.alloc((S_LEN,), BF16)
            KTa = Y.alloc((S_LEN,), BF16)
            KTb = Y.alloc((S_LEN,), BF16)
            R_QTa = [Res() for _ in range(4)]
            R_QTb = [Res() for _ in range(4)]
            R_KTa = [Res() for _ in range(4)]
            R_KTb = [Res() for _ in range(4)]
            ch_ind = new_chan()
            VA = Y.alloc((16, 128), BF16)
            VB = Y.alloc((16, 128), BF16)
            R_VA = [Res() for _ in range(4)]
            R_VB = [Res() for _ in range(4)]
            PT = [Y.alloc((512,), BF16) for _ in range(6)]
            R_PT = [Res() for _ in range(6)]
            pt_rot = Rot([0, 1, 2, 3, 4, 5])
            ubf = [Y.alloc((512,), BF16) for _ in range(2)]
            t1 = [Y.alloc((512,), F32) for _ in range(2)]
            t2 = [Y.alloc((512,), F32) for _ in range(2)]
            R_ubf = [Res(), Res()]
            R_t1 = [Res(), Res()]
            R_t2 = [Res(), Res()]
            lnbuf = [Y.alloc((512,), F32) for _ in range(2)]
            rcp = [Y.alloc((512,), F32) for _ in range(2)]
            R_rcp = [Res(), Res()]
            a1 = Y.alloc((512,), F32)
            a2 = Y.alloc((512,), F32)
            ob = [Y.alloc((512,), F32) for _ in range(2)]
            sqo = [Y.alloc((512,), BF16) for _ in range(2)]
            rst = [Y.alloc((512,), F32) for _ in range(2)]
            R_a1, R_a2 = Res(), Res()
            R_ob = [Res(), Res()]
            R_sqo = [Res(), Res()]
            R_rst = [Res(), Res()]
            Gt = Y.alloc((32, 8), F32)
            cmp = Y.alloc((32, 8, 8), BF16)
            rank = Y.alloc((32, 8), F32)
            kmf = Y.alloc((8,), F32)
            kmT = Y.alloc((8,), BF16)
            R_gate = Res()
            R_km = Res()

            S.add("pool", lambda e: e.memset(QTa, 0.0), writes=R_QTa)
            S.add("pool", lambda e: e.memset(QTb, 0.0), writes=R_QTb)
            S.add("pool", lambda e: e.memset(KTa[64:128, :], 0.0), writes=R_KTa)
            S.add("pool", lambda e: e.memset(KTb[0:64, :], 0.0), writes=R_KTb)

            jobs = [("diff", h) for h in range(4)] + [("moba", jp) for jp in range(4)]

            def job_cols(job):
                kind, i = job
                if kind == "diff":
                    return 1024 + i * 128, 1536 + i * 128, 2048 + i * 128
                return 2560 + i * 128, 3072 + i * 128, 3584 + i * 128

            def load_job_weights(ji):
                s = ji % 2
                cq, ck, cv = job_cols(jobs[ji])
                dma("pool", wq[s], w_in_l[:, :, cq:cq + 128], writes=[R_wj[s]], chan=ch_wj[s])
                dma("pool", wk[s], w_in_l[:, :, ck:ck + 128], writes=[R_wj[s]], chan=ch_wj[s])
                dma("pool", wv[s], w_in_l[:, :, cv:cv + 128], writes=[R_wj[s]], chan=ch_wj[s])

            assert Y.off <= YT0, (Y.off, YT0)
            load_job_weights(0)
            on_pairs = Rot([(3, 4), (5, 6)])
            s_pool = Rot([0, 1, 2])
            pt_rot = Rot([0, 1, 2, 3, 4, 5])

            for ji, job in enumerate(jobs):
                kind, idx = job
                s = ji % 2
                if ji + 1 < len(jobs):
                    load_job_weights(ji + 1)
                if kind == "moba" and idx == 0:
                    dma("pool", KTa[64:72, :], ind_d, writes=R_KTa, chan=ch_ind)
                    dma("pool", KTb[0:8, :], ind_d, writes=R_KTb, chan=ch_ind)
                    S.add("pool", lambda e: e.memset(VA[:, :, 64:128], 1.0), writes=R_VA)
                    S.add("pool", lambda e: e.memset(VB[:, :, 0:64], 1.0), writes=R_VB)
                tiles = [("q", wq[s], tc) for tc in range(4)] + [("k", wk[s], tc) for tc in range(4)]
                pend = None

                def rope_tail(p):
                    bU, u2, which, tc = p
                    bR = aux_pool.next()
                    mm(banks[bR], Rm, ubf[u2], True, True, [R_cst, R_ubf[u2]], [Rb[bR]])
                    S.add("dve", lambda e: e.tensor_tensor(t1[u2], banks[bU], cosT[:, tc * 512:(tc + 1) * 512], ALU.mult),
                          reads=[Rb[bU], R_trig], writes=[R_t1[u2]])
                    S.add("dve", lambda e: e.tensor_tensor(t2[u2], banks[bR], sinT[:, tc * 512:(tc + 1) * 512], ALU.mult),
                          reads=[Rb[bR], R_trig], writes=[R_t2[u2]])
                    cs = slice(tc * 512, (tc + 1) * 512)
                    if which == "q" and kind == "diff":
                        S.add("dve", lambda e: e.tensor_tensor(QTa[:, cs], t1[u2], t2[u2], ALU.add),
                              reads=[R_t1[u2], R_t2[u2]], writes=[R_QTa[tc]])
                    else:
                        da, db = (QTa, QTb) if which == "q" else (KTa, KTb)
                        Ra, Rb_ = (R_QTa, R_QTb) if which == "q" else (R_KTa, R_KTb)
                        S.add("dve", lambda e: e.tensor_tensor(da[0:64, cs], t1[u2][0:64, :], t2[u2][0:64, :], ALU.add),
                              reads=[R_t1[u2], R_t2[u2]], writes=[Ra[tc]])
                        S.add("dve", lambda e: e.tensor_tensor(db[64:128, cs], t1[u2][64:128, :], t2[u2][64:128, :], ALU.add),
                              reads=[R_t1[u2], R_t2[u2]], writes=[Rb_[tc]])

                if ji == 0 and os.environ.get("DBG_NOINT"):
                    for tcc in range(4):
                        convln_p1(tcc)
                        convln_p2(tcc)
                elif ji == 0:
                    convln_p1(0)
                for ti, (which, w, tc) in enumerate(tiles):
                    if ji == 0 and 1 <= ti <= 4 and not os.environ.get("DBG_NOINT"):
                        convln_p2(ti - 1)
                        if ti < 4:
                            convln_p1(ti)
                    bU = proj_pool.next()
                    for k in range(8):
                        mm(banks[bU], w[:, k, :], xT[:, k, tc * 512:(tc + 1) * 512], k == 0, k == 7,
                           [R_wj[s]] + R_xT[tc * 4:tc * 4 + 4], [Rb[bU]])
                    u2 = ti % 2
                    S.add("act", lambda e, bU=bU, u2=u2: e.copy(ubf[u2], banks[bU]), reads=[Rb[bU]], writes=[R_ubf[u2]])
                    if pend is not None:
                        rope_tail(pend)
                    pend = (bU, u2, which, tc)
                for g4 in range(4):
                    bV = aux_pool.next()
                    for t4 in range(4):
                        tt = g4 * 4 + t4
                        for k in range(8):
                            mm(banks[bV][:, t4 * 128:(t4 + 1) * 128], xT[:, k, tt * 128:(tt + 1) * 128], wv[s][:, k, :],
                               k == 0, k == 7, [R_wj[s], R_xT[tt]], [Rb[bV]])
                    if g4 == 0:
                        rope_tail(pend)
                        pend = None
                    pv3 = banks[bV].rearrange("p (a b) -> p a b", b=128)
                    if kind == "diff":
                        S.add("dve", lambda e, g4=g4, pv3=pv3: e.tensor_copy(VA[:, g4 * 4:(g4 + 1) * 4, :], pv3),
                              reads=[Rb[bV]], writes=[R_VA[g4]])
                    else:
                        S.add("dve", lambda e, g4=g4, pv3=pv3: e.tensor_copy(VA[:, g4 * 4:(g4 + 1) * 4, 0:64], pv3[:, :, 0:64]),
                              reads=[Rb[bV]], writes=[R_VA[g4]])
                        S.add("dve", lambda e, g4=g4, pv3=pv3: e.tensor_copy(VB[:, g4 * 4:(g4 + 1) * 4, 64:128], pv3[:, :, 64:128]),
                              reads=[Rb[bV]], writes=[R_VB[g4]])
                if kind == "moba":
                    S.add("dve", lambda e: e.tensor_reduce(kmf[0:64, :], KTa[0:64, :].rearrange("p (n j) -> p n j", j=256), AX.X, ALU.add),
                          reads=R_KTa, writes=[R_km])
                    S.add("dve", lambda e: e.tensor_reduce(kmf[64:128, :], KTb[64:128, :].rearrange("p (n j) -> p n j", j=256), AX.X, ALU.add),
                          reads=R_KTb, writes=[R_km])
                    S.add("dve", lambda e: e.tensor_scalar(kmT, kmf, 1.0 / 256, None, ALU.mult), reads=[R_km], writes=[R_km])
                    for qt in range(16):
                        for hl in range(2):
                            Qs = QTa if hl == 0 else QTb
                            Rq = R_QTa if hl == 0 else R_QTb
                            mm(banks[7][:, qt * 16 + hl * 8:qt * 16 + hl * 8 + 8],
                               Qs[hl * 64:(hl + 1) * 64, qt * 128:(qt + 1) * 128], kmT[hl * 64:(hl + 1) * 64, :],
                               True, True, [Rq[qt // 4], R_km], [Rb[7]])
                    G4 = Gt.rearrange("p (a h) n -> p a h n", h=2)
                    S.add("dve", lambda e, G4=G4: e.tensor_tensor(
                        G4, banks[7][:, 0:256].rearrange("p (a h n) -> p a h n", h=2, n=8),
                        vmask[:, 0, :, :].rearrange("p a (o n) -> p a o n", o=1).broadcast_to([128, 16, 2, 8]), ALU.add),
                        reads=[Rb[7], R_cst], writes=[R_gate])
                    ga = Gt.rearrange("p a (o n) -> p a o n", o=1).broadcast_to([128, 32, 8, 8])
                    gb = Gt.rearrange("p a (n o) -> p a n o", o=1).broadcast_to([128, 32, 8, 8])
                    S.add("dve", lambda e, ga=ga, gb=gb: e.tensor_tensor(cmp, ga, gb, ALU.is_gt), reads=[R_gate], writes=[R_gate])
                    S.add("dve", lambda e: e.tensor_reduce(rank, cmp, AX.X, ALU.add), reads=[R_gate], writes=[R_gate])
                    R4 = rank.rearrange("p (a h) n -> p a h n", h=2)
                    for hl in range(2):
                        c0 = 64 if hl == 0 else 0
                        S.add("dve", lambda e, hl=hl, R4=R4, c0=c0: e.scalar_tensor_tensor(
                            biaspad[:, :, c0:c0 + 8], R4[:, :, hl, :], 3.0, vmask[:, 1, :, :], ALU.is_ge, ALU.mult),
                            reads=[R_gate, R_cst], writes=[R_biaspad])
                    for q4 in range(4):
                        bB = aux_pool.next()
                        for t4 in range(4):
                            qt = q4 * 4 + t4
                            mm(banks[bB][0:72, t4 * 128:(t4 + 1) * 128], biaspad[:, qt, :], ident, True, True,
                               [R_biaspad, R_cst], [Rb[bB]])
                        S.add("dve", lambda e, bB=bB, q4=q4: e.tensor_copy(QTa[64:72, q4 * 512:(q4 + 1) * 512], banks[bB][64:72, :]),
                              reads=[Rb[bB]], writes=[R_QTa[q4]])
                        S.add("dve", lambda e, bB=bB, q4=q4: e.tensor_copy(QTb[0:8, q4 * 512:(q4 + 1) * 512], banks[bB][0:8, :]),
                              reads=[Rb[bB]], writes=[R_QTb[q4]])

                pend_pv = []
                deferred = []

                def emit_pv(tk):
                    (pslot, w, q0, pvs, first, last, after) = tk
                    for (bank, lhsT, Rl) in pvs:
                        mm(banks[bank][:, q0:q0 + w], lhsT, PT[pslot][:, 0:w], first, last,
                           [R_PT[pslot]] + Rl, [Rb[bank]])
                    if after is not None:
                        after()

                def push(kAP, qAP, Rk, Rq, masks, w, q0, pvs, first, last, after):
                    bS = s_pool.next()
                    nm = len(masks)
                    mm(banks[bS][:, 0:w], kAP, qAP, True, nm == 0, Rk + Rq, [Rb[bS]])
                    for mi, (ml, mr, c0, c1, Rm_) in enumerate(masks):
                        mm(banks[bS][:, c0:c1], ml, mr, False, mi == nm - 1, Rm_, [Rb[bS]])
                    pslot = pt_rot.next()
                    S.add("act", lambda e, bS=bS, pslot=pslot, w=w: e.activation(PT[pslot][:, 0:w], banks[bS][:, 0:w], AF.Exp, scale=0.125),
                          reads=[Rb[bS]], writes=[R_PT[pslot]])
                    pend_pv.append((pslot, w, q0, pvs, first, last, after))
                    while len(pend_pv) > 3:
                        emit_pv(pend_pv.pop(0))

                for qc in range(4):
                    if kind == "diff":
                        h = idx
                        for t in range(2):
                            bO, bN = on_pairs.next()
                            nk = 4 * qc + 4
                            rs = t % 2

                            def fin(t=t, bO=bO, bN=bN, qc=qc, h=h, rs=rs):
                                S.add("act", lambda e: e.activation(lnbuf[rs], banks[bN], AF.Ln), reads=[Rb[bN]], writes=[R_rcp[rs]])
                                S.add("act", lambda e: e.activation(rcp[rs], lnbuf[rs], AF.Exp, scale=-1.0), reads=[R_rcp[rs]], writes=[R_rcp[rs]])
                                if t == 0:
                                    S.add("dve", lambda e: e.tensor_tensor(a1, banks[bO], rcp[rs], ALU.mult),
                                          reads=[Rb[bO], R_rcp[rs]], writes=[R_a1])
                                else:
                                    o2 = qc % 2
                                    S.add("dve", lambda e: e.tensor_tensor(a2, banks[bO], rcp[rs], ALU.mult),
                                          reads=[Rb[bO], R_rcp[rs]], writes=[R_a2])
                                    S.add("dve", lambda e: e.scalar_tensor_tensor(ob[o2], a2, neglam, a1, ALU.mult, ALU.add),
                                          reads=[R_a1, R_a2, R_lam], writes=[R_ob[o2]])
                                    S.add("dve", lambda e: e.tensor_tensor(sqo[o2], ob[o2], ob[o2], ALU.mult), reads=[R_ob[o2]], writes=[R_sqo[o2]])

                                    def rms_tail():
                                        mm(banks[7], ones, sqo[o2], True, True, [R_cst, R_sqo[o2]], [Rb[7]])
                                        S.add("act", lambda e: e.activation(rst[o2], banks[7], AF.Ln, bias=128.0 * EPS, scale=1.0),
                                              reads=[Rb[7]], writes=[R_rst[o2]])
                                        S.add("act", lambda e: e.activation(rst[o2], rst[o2], AF.Exp, scale=-0.5), reads=[R_rst[o2]], writes=[R_rst[o2]])
                                        S.add("dve", lambda e: e.scalar_tensor_tensor(
                                            oT[1][:, h, qc * 512:(qc + 1) * 512], ob[o2], gsc, rst[o2], ALU.mult, ALU.mult),
                                            reads=[R_ob[o2], R_rst[o2], R_lam], writes=[R_oT[1][h][qc]])
                                    deferred.append(rms_tail)

                            Ks, Rk_ = (KTa, R_KTa) if t == 0 else (KTb, R_KTb)
                            for kt in range(nk):
                                j = kt - 4 * qc
                                q0 = 128 * j if j > 0 else 0
                                w = 512 - q0
                                masks = []
                                if j >= 0:
                                    masks.append((ident, tri, 0, 128, [R_cst]))
                                pvs = [(bO, VA[:, kt, :], [R_VA[kt // 4]]), (bN, ones, [R_cst])]
                                first = (kt == 0)
                                last = (kt == nk - 1)
                                after = None
                                if last:
                                    def after(fin=fin):
                                        fin()
                                        while len(deferred) > 1:
                                            deferred.pop(0)()
                                push(Ks[:, kt * 128:(kt + 1) * 128], QTa[:, qc * 512 + q0:qc * 512 + q0 + w],
                                     [Rk_[kt // 4]], [R_QTa[qc]], masks, w, q0, pvs, first, last, after)
                    else:
                        jp = idx
                        bA, bB2 = on_pairs.next()
                        nk = 4 * qc + 4
                        rs = qc % 2

                        def finm(bA=bA, bB2=bB2, qc=qc, jp=jp, rs=rs):
                            S.add("act", lambda e: e.activation(lnbuf[rs][0:64, :], banks[bA][64:128, :], AF.Ln), reads=[Rb[bA]], writes=[R_rcp[rs]])
                            S.add("act", lambda e: e.activation(lnbuf[rs][64:128, :], banks[bB2][0:64, :], AF.Ln), reads=[Rb[bB2]], writes=[R_rcp[rs]])
                            S.add("act", lambda e: e.activation(rcp[rs], lnbuf[rs], AF.Exp, scale=-1.0), reads=[R_rcp[rs]], writes=[R_rcp[rs]])
                            cs = slice(qc * 512, (qc + 1) * 512)
                            S.add("dve", lambda e: e.tensor_tensor(oT[2][0:64, jp, cs], banks[bA][0:64, :], rcp[rs][0:64, :], ALU.mult),
                                  reads=[Rb[bA], R_rcp[rs]], writes=[R_oT[2][jp][qc]])
                            S.add("dve", lambda e: e.tensor_tensor(oT[2][64:128, jp, cs], banks[bB2][64:128, :], rcp[rs][64:128, :], ALU.mult),
                                  reads=[Rb[bB2], R_rcp[rs]], writes=[R_oT[2][jp][qc]])

                        for hl in range(2):
                            Ks, Rk_ = (KTa, R_KTa) if hl == 0 else (KTb, R_KTb)
                            Qs, Rq_ = (QTa, R_QTa) if hl == 0 else (QTb, R_QTb)
                            for kt in range(nk):
                                j = kt - 4 * qc
                                q0 = 128 * j if j > 0 else 0
                                w = 512 - q0
                                masks = []
                                if j >= 0:
                                    masks.append((ident, tri, 0, 128, [R_cst]))
                                if hl == 0:
                                    pvs = [(bA, VA[:, kt, :], [R_VA[kt // 4]])]
                                else:
                                    pvs = [(bB2, VB[:, kt, :], [R_VB[kt // 4]])]
                                first = (kt == 0)
                                last = (kt == nk - 1)
                                push(Ks[:, kt * 128:(kt + 1) * 128], Qs[:, qc * 512 + q0:qc * 512 + q0 + w],
                                     [Rk_[kt // 4]], [Rq_[qc]], masks, w, q0, pvs, first, last,
                                     finm if (last and hl == 1) else None)
                while pend_pv:
                    emit_pv(pend_pv.pop(0))
                while deferred:
                    deferred.pop(0)()
            S.barrier()
            if stop == 2:
                break

            Y = region(Y0)
            mT = Y.alloc((8, S_LEN), BF16)
            R_mT = [[Res() for _ in range(4)] for _ in range(8)]
            wo = Y.alloc((8, D), BF16)
            R_wo = Res()
            ch_wo = new_chan()
            YB = Y.off
            wout = [Y.alloc((4, D), BF16) for _ in range(3)]
            R_wout = Res()
            ch_wout = new_chan()
            wg = [[Y.alloc((8, 128), BF16) for _ in range(3)] for _ in range(2)]
            R_wg = [Res(), Res()]
            ch_wg = [new_chan(), new_chan()]
            gtt = [Y.alloc((512,), F32) for _ in range(3)]
            R_gt = [Res() for _ in range(3)]
            macc = [Y.alloc((512,), F32) for _ in range(2)]
            tmpb = [Y.alloc((512,), F32) for _ in range(2)]
            R_macc = [Res(), Res()]
            R_tmp = [Res(), Res()]
            for br, wd in enumerate((w_co_d, w_do_d, w_mo_d)):
                dma("pool", wout[br], wd[l].rearrange("(k p) n -> p k n", p=128), writes=[R_wout], chan=ch_wout)

            def load_wg(r):
                s = r % 2
                for br in range(3):
                    c0 = 4096 + br * 1024 + r * 128
                    dma("pool", wg[s][br], w_in_l[:, :, c0:c0 + 128], writes=[R_wg[s]], chan=ch_wg[s])

            load_wg(0)
            w_o_l = w_o_d[l].rearrange("(k p) n -> p k n", p=128)
            dma("pool", wo[:, 0:4, :], w_o_l[:, 0:4, :], writes=[R_wo], chan=ch_wo)
            dma("pool", wo[:, 4:8, :], w_o_l[:, 4:8, :], writes=[R_wo], chan=ch_wo)
            all_pool = Rot([0, 1, 2, 3, 4, 5, 6, 7])
            it = 0
            for r in range(8):
                s = r % 2
                if r + 1 < 8:
                    load_wg(r + 1)
                for tc in range(4):
                    m2 = it % 2
                    it += 1
                    for br in range(3):
                        bY = all_pool.next()
                        for k in range(4):
                            mm(banks[bY], wout[br][:, k, r * 128:(r + 1) * 128], oT[br][:, k, tc * 512:(tc + 1) * 512],
                               k == 0, k == 3, [R_wout, R_oT[br][k][tc]], [Rb[bY]])
                        bG = all_pool.next()
                        for k in range(8):
                            mm(banks[bG], wg[s][br][:, k, :], xT[:, k, tc * 512:(tc + 1) * 512], k == 0, k == 7,
                               [R_wg[s]] + R_xT[tc * 4:tc * 4 + 4], [Rb[bG]])
                        S.add("act", lambda e, bG=bG, br=br, r=r: e.activation(
                            gtt[br], banks[bG], AF.Sigmoid, bias=bgate[:, br * 8 + r:br * 8 + r + 1], scale=1.0),
                            reads=[Rb[bG], R_small], writes=[R_gt[br]])
                        if br == 0:
                            S.add("dve", lambda e, bY=bY, m2=m2: e.tensor_tensor(macc[m2], banks[bY], gtt[0], ALU.mult),
                                  reads=[Rb[bY], R_gt[0]], writes=[R_macc[m2]])
                        elif br == 1:
                            S.add("dve", lambda e, bY=bY, m2=m2: e.tensor_tensor(tmpb[m2], banks[bY], gtt[1], ALU.mult),
                                  reads=[Rb[bY], R_gt[1]], writes=[R_tmp[m2]])
                            S.add("pool", lambda e, m2=m2: e.tensor_tensor(macc[m2], macc[m2], tmpb[m2], ALU.add),
                                  reads=[R_macc[m2], R_tmp[m2]], writes=[R_macc[m2]])
                        else:
                            S.add("dve", lambda e, bY=bY, m2=m2: e.tensor_tensor(tmpb[m2], banks[bY], gtt[2], ALU.mult),
                                  reads=[Rb[bY], R_gt[2]], writes=[R_tmp[m2]])
                            S.add("dve", lambda e, m2=m2, r=r, tc=tc: e.tensor_tensor(
                                mT[:, r, tc * 512:(tc + 1) * 512], macc[m2], tmpb[m2], ALU.add),
                                reads=[R_macc[m2], R_tmp[m2]], writes=[R_mT[r][tc]])
            S.barrier()
            if stop == 3:
                break

            Y = region(YB)
            lng = Y.alloc((D,), F32)
            lnb = Y.alloc((D,), F32)
            R_ln = Res()
            ch_ln = new_chan()
            W1 = [Y.alloc((8, 512), BF16) for _ in range(2)]
            R_W1 = [Res(), Res()]
            ch_W1 = [new_chan(), new_chan()]
            W2 = [Y.alloc((4, D), BF16) for _ in range(2)]
            R_W2 = [Res(), Res()]
            ch_W2 = [new_chan(), new_chan()]
            xbs = [Y.alloc((D,), BF16) for _ in range(2)]
            R_xbs = [Res(), Res()]
            xin = [Y.alloc((D,), F32) for _ in range(2)]
            R_xin = [Res(), Res()]
            ch_xin = [new_chan(), new_chan()]
            dma("sp", lng, ln_d[l, 0].partition_broadcast(128), writes=[R_ln], chan=ch_ln)
            dma("sp", lnb, ln_d[l, 1].partition_broadcast(128), writes=[R_ln], chan=ch_ln)
            w1_l = w_f1_d[l].rearrange("(k p) n -> p k n", p=128)
            w2_l = w_f2_d[l].rearrange("(k p) n -> p k n", p=128)

            def load_w1(i):
                s = i % 2
                dma("pool", W1[s], w1_l[:, :, i * 512:(i + 1) * 512], writes=[R_W1[s]], chan=ch_W1[s])

            def load_w2(i):
                s = i % 2
                dma("pool", W2[s], w2_l[:, i * 4:(i + 1) * 4, :], writes=[R_W2[s]], chan=ch_W2[s])

            load_w1(0)
            load_w1(1)
            load_w2(0)
            load_w2(1)
            xsrc = x_d if l == 0 else y_d
            deferred = []
            for tt in range(NT):
                s = tt % 2
                dma("sp", xin[s], xsrc[tt * 128:(tt + 1) * 128, :], reads=[R_y[tt]], writes=[R_xin[s]], chan=ch_xin[s])
                for half in range(2):
                    bW = proj_pool.next()
                    for k in range(8):
                        mm(banks[bW], mT[:, k, tt * 128:(tt + 1) * 128], wo[:, k, half * 512:(half + 1) * 512], k == 0, k == 7,
                           [R_mT[k][tt // 4], R_wo], [Rb[bW]])
                    S.add("dve", lambda e, bW=bW, tt=tt, half=half, s=s: e.scalar_tensor_tensor(
                        x_res[:, tt, half * 512:(half + 1) * 512], xin[s][:, half * 512:(half + 1) * 512], ALPHA, banks[bW],
                        ALU.mult, ALU.add), reads=[Rb[bW], R_xin[s]], writes=[R_xres[tt]])
                old = list(deferred)
                layer_norm_tile(tt, lng, lnb, R_ln, False, True, xbs, R_xbs, deferred)
                run_pipe(old)
                deferred[:] = [it for it in deferred if it]
            run_pipe(deferred, drain=True)
            S.barrier()
            if stop == 4:
                break

            Y = region(Y0)
            hTg = Y.alloc((8, S_LEN), BF16)
            R_hTg = [[Res() for _ in range(4)] for _ in range(8)]
            rl = [Y.alloc((512,), F32) for _ in range(2)]
            R_rl = [Res(), Res()]
            assert Y.off <= YB
            dma("sp", lng, ln_d[l, 2].partition_broadcast(128), writes=[R_ln], chan=ch_ln)
            dma("sp", lnb, ln_d[l, 3].partition_broadcast(128), writes=[R_ln], chan=ch_ln)
            deferred = []
            it = 0
            for g in range(4):
                for hf in range(2):
                    i = g * 2 + hf
                    s = i % 2
                    for fc4 in range(4):
                        fc = hf * 4 + fc4
                        for tc in range(4):
                            bH = proj_pool.next() if False else all_pool.next()
                            for k in range(8):
                                mm(banks[bH], W1[s][:, k, fc4 * 128:(fc4 + 1) * 128], xT[:, k, tc * 512:(tc + 1) * 512],
                                   k == 0, k == 7, [R_W1[s]] + R_xT[tc * 4:tc * 4 + 4], [Rb[bH]])
                            r2 = it % 2
                            it += 1
                            bcol = bff1[:, g * 8 + fc:g * 8 + fc + 1]
                            S.add("act", lambda e, bH=bH, r2=r2, bcol=bcol: e.activation(rl[r2], banks[bH], AF.Relu, bias=bcol, scale=1.0),
                                  reads=[Rb[bH], R_small], writes=[R_rl[r2]])
                            S.add("dve", lambda e, bH=bH, r2=r2, bcol=bcol, fc=fc, tc=tc: e.scalar_tensor_tensor(
                                hTg[:, fc, tc * 512:(tc + 1) * 512], banks[bH], bcol, rl[r2], ALU.add, ALU.mult),
                                reads=[Rb[bH], R_rl[r2], R_small], writes=[R_hTg[fc][tc]])
                    if i + 2 < 8:
                        load_w1(i + 2)
                for tt in range(NT):
                    for half in range(2):
                        bF = all_pool.next()
                        if g == 0:
                            mm(banks[bF], ones[0:1, :], b2row[0:1, half * 512:(half + 1) * 512], True, False,
                               [R_cst, R_b2], [Rb[bF]])
                        for fc in range(8):
                            mm(banks[bF], hTg[:, fc, tt * 128:(tt + 1) * 128], W2[fc // 4][:, fc % 4, half * 512:(half + 1) * 512],
                               (fc == 0 and g != 0), fc == 7, [R_hTg[fc][tt // 4], R_W2[fc // 4]], [Rb[bF]])
                        xr = x_res[:, tt, half * 512:(half + 1) * 512]
                        if g == 0:
                            S.add("dve", lambda e, bF=bF, xr=xr: e.scalar_tensor_tensor(xr, xr, ALPHA, banks[bF], ALU.mult, ALU.add),
                                  reads=[Rb[bF], R_xres[tt]], writes=[R_xres[tt]])
                        else:
                            S.add("dve", lambda e, bF=bF, xr=xr: e.tensor_tensor(xr, xr, banks[bF], ALU.add),
                                  reads=[Rb[bF], R_xres[tt]], writes=[R_xres[tt]])
                    if g == 3:
                        old = list(deferred)
                        layer_norm_tile(tt, lng, lnb, R_ln, True, not last_layer, xbs, R_xbs, deferred)
                        run_pipe(old)
                        deferred[:] = [it for it in deferred if it]
                if g < 3:
                    load_w2(2 * (g + 1))
                    load_w2(2 * (g + 1) + 1)
            run_pipe(deferred, drain=True)
            S.barrier()

        S.finalize()
        with nc.Block() as block:
            @block.tensor
            def _(e):
                S.emit("pe", e, sems)

            @block.scalar
            def _(e):
                S.emit("act", e, sems)

            @block.vector
            def _(e):
                S.emit("dve", e, sems)

            @block.gpsimd
            def _(e):
                S.emit("pool", e, sems)

            @block.sync
            def _(e):
                S.emit("sp", e, sems, final_waits=[ch_y])
    return nc


def _constants():
    pos = np.arange(S_LEN, dtype=np.float32)
    inv = (np.float32(10000.0) ** (-np.arange(0, 64, 2, dtype=np.float32) / np.float32(64))).astype(np.float32)
    ang = (pos[:, None] * inv[None, :]).astype(np.float32)
    ang = np.concatenate([ang, ang], axis=-1)
    cos = np.cos(ang).astype(np.float32).T
    sin = np.sin(ang).astype(np.float32).T
    sgn = np.where(np.arange(64) < 32, -1.0, 1.0).astype(np.float32)[:, None]
    cos_t = np.concatenate([cos, cos], axis=0)
    sin_t = np.concatenate([sin * sgn, sin * sgn], axis=0)
    ident = np.eye(128, dtype=np.float32)
    Rm = np.zeros((128, 128), np.float32)
    for j in range(128):
        b, o = divmod(j, 64)
        Rm[b * 64 + (o + 32) % 64, j] = 1.0
    k = np.arange(128)[:, None]
    q = np.arange(128)[None, :]
    tri = np.where(k > q, MASKV, 0.0).astype(np.float32)
    ones = np.ones((128, 128), np.float32)
    onesE = np.zeros((128, 128), np.float32)
    onesE[:, :64] = 1.0
    onesO = np.zeros((128, 128), np.float32)
    onesO[:, 64:] = 1.0
    cst = np.concatenate([ident, Rm, tri, ones, onesE, onesO], axis=1)
    ind = np.zeros((8, S_LEN), np.float32)
    blk = np.arange(S_LEN) // 256
    for n in range(8):
        ind[n, blk == n] = 1.0
    own = (np.arange(16) // 2)[:, None]
    n = np.arange(8)[None, :]
    vneg = np.where(n < own, 0.0, -1.0e30).astype(np.float32)
    vm = np.where(n < own, MASKV, 0.0).astype(np.float32)
    vmask = np.concatenate([vneg.reshape(-1), vm.reshape(-1)])[None, :].repeat(128, axis=0)
    return dict(cos_t=np.ascontiguousarray(cos_t), sin_t=np.ascontiguousarray(sin_t), cst=np.ascontiguousarray(cst),
                ind_h=ind, vmask=np.ascontiguousarray(vmask.astype(np.float32)))


def _prep_inputs(inp):
    f = lambda a: np.ascontiguousarray(np.asarray(a, dtype=np.float32))
    L = DEPTH
    shared = {}
    for k in ("w_in", "w_conv_out", "w_diff_out", "w_moba_out", "w_o", "w_ff1", "w_ff2", "b_ff2"):
        shared[k] = f(inp[k])
    shared["b_gate_t"] = f(np.asarray(inp["b_gate"]).reshape(L, 24, 128).transpose(0, 2, 1))
    cw = np.asarray(inp["conv_w"]).reshape(L, 31, 4, 128)
    shared["conv_w_t"] = f(cw.transpose(0, 3, 2, 1).reshape(L, 128, 4 * 31))
    cp = np.stack([np.asarray(inp["conv_b"]), np.asarray(inp["conv_ln_g"]), np.asarray(inp["conv_ln_b"])], axis=1)
    shared["conv_p_t"] = f(cp.reshape(L, 3, 4, 128).transpose(0, 3, 1, 2).reshape(L, 128, 12))
    shared["lam_all"] = f(np.concatenate([np.asarray(inp[k]) for k in ("lam_q1", "lam_k1", "lam_q2", "lam_k2")], axis=1))
    shared["subln_g_t"] = f(np.asarray(inp["diff_subln_g"]).reshape(L, 128, 1))
    shared["ln_all"] = f(np.stack([np.asarray(inp[k]) for k in ("ln1_g", "ln1_b", "ln2_g", "ln2_b")], axis=1))
    shared["b_ff1_t"] = f(np.asarray(inp["b_ff1"]).reshape(L, 32, 128).transpose(0, 2, 1))
    shared.update(_constants())
    return shared


_NC_CACHE = {}


def kernel(**inputs):
    x = np.ascontiguousarray(np.asarray(inputs["x"], dtype=np.float32))
    shared = _prep_inputs(inputs)
    if "nc" not in _NC_CACHE:
        _NC_CACHE["nc"] = build_program(DEPTH)
    nc = _NC_CACHE["nc"]
    in_maps = []
    for b in range(8):
        m = dict(shared)
        m["x"] = x[b]
        in_maps.append(m)
    res = run_bass_kernel_spmd(nc, in_maps, core_ids=list(range(8)))
    return np.stack([np.asarray(r["y"], dtype=np.float32) for r in res.results], axis=0)
```
